# Optimizing a Trainium2 kernel written in Bass

```python
import math
import jax, jax.numpy as jnp
from jax import lax
import numpy as np


D_MODEL = 2048
BATCH = 1
SEQ = 16384
DEPTH = 2

HEAD_DIM = 128
RET_HEADS = 6
MLSTM_HEADS = 6
RET_WIDTH = RET_HEADS * HEAD_DIM
MLSTM_WIDTH = MLSTM_HEADS * HEAD_DIM
S5_WIDTH = D_MODEL - RET_WIDTH - MLSTM_WIDTH
S5_GROUP = 16
S5_GROUPS = S5_WIDTH // S5_GROUP
S5_STATE = 64
CONV_WIDTH = 4
CHUNK = 128
D_FF = 4 * D_MODEL
PLE_DIM = 256
ROPE_BASE = 10000.0
LN_EPS = 1e-5
ALPHA = (2 * DEPTH) ** 0.25
BETA = (8 * DEPTH) ** -0.25
IN_SPLITS = tuple(int(v) for v in np.cumsum([RET_WIDTH] * 4 + [MLSTM_WIDTH] * 4 + [MLSTM_HEADS] * 2))
IN_WIDTH = IN_SPLITS[-1] + S5_WIDTH

kernel_name = 'hybrid_retention_mlstm_s5_block'


def layer_norm(x, w, b):
    xf = x.astype(jnp.float32)
    mu = jnp.mean(xf, -1, keepdims=True)
    var = jnp.mean(jnp.square(xf - mu), -1, keepdims=True)
    return (xf - mu) * lax.rsqrt(var + LN_EPS) * w.astype(jnp.float32) + b.astype(jnp.float32)


def head_norm(y, w):
    mu = jnp.mean(y, -1, keepdims=True)
    var = jnp.mean(jnp.square(y - mu), -1, keepdims=True)
    yn = (y - mu) * lax.rsqrt(var + LN_EPS)
    return yn.reshape(y.shape[0], y.shape[1], -1) * w.astype(jnp.float32)


def apply_rotary(t, cos, sin):
    t1, t2 = jnp.split(t, 2, axis=-1)
    return jnp.concatenate([t1 * cos - t2 * sin, t1 * sin + t2 * cos], -1)


def to_chunks(t):
    b, l, h = t.shape[:3]
    t = t.reshape((b, l // CHUNK, CHUNK, h) + t.shape[3:])
    return t.transpose((1, 0, 3, 2) + tuple(range(4, t.ndim)))


def from_chunks(t):
    nc, b, h, c, d = t.shape
    return t.transpose(1, 0, 3, 2, 4).reshape(b, nc * c, h, d)


def retention(q, k, v):
    lg = jnp.log(1.0 - 2.0 ** (-5.0 - jnp.arange(RET_HEADS, dtype=jnp.float32)))
    idx = jnp.arange(CHUNK, dtype=jnp.float32)
    rel = idx[:, None] - idx[None, :]
    decay = jnp.where(rel >= 0, jnp.exp(lg[:, None, None] * jnp.maximum(rel, 0.0)), 0.0)
    q_dec = jnp.exp(lg[:, None] * (idx + 1.0))[:, :, None]
    k_dec = jnp.exp(lg[:, None] * (CHUNK - 1.0 - idx))[:, :, None]
    c_dec = jnp.exp(lg * CHUNK)[:, None, None]

    def step(state, inp):
        qc, kc, vc = inp
        scores = jnp.einsum('bhid,bhjd->bhij', qc, kc) * decay
        out = (jnp.einsum('bhij,bhje->bhie', scores, vc)
               + jnp.einsum('bhid,bhde->bhie', qc, state) * q_dec)
        state = state * c_dec + jnp.einsum('bhjd,bhje->bhde', kc * k_dec, vc)
        return state, out

    state0 = jnp.zeros((q.shape[0], RET_HEADS, HEAD_DIM, HEAD_DIM), jnp.float32)
    _, out = lax.scan(step, state0, (to_chunks(q), to_chunks(k), to_chunks(v)))
    return from_chunks(out)


def mlstm(q, k, v, log_i, log_f):
    causal = jnp.tril(jnp.ones((CHUNK, CHUNK), bool))

    def step(carry, inp):
        c_st, n_st, m_st = carry
        qc, kc, vc, ic, fc = inp
        bcum = jnp.cumsum(fc, axis=-1)
        log_d = jnp.where(causal, bcum[..., :, None] - bcum[..., None, :] + ic[..., None, :], -jnp.inf)
        log_inter = bcum + m_st[..., None]
        m_row = jnp.maximum(jnp.max(log_d, -1), log_inter)
        s = jnp.einsum('bhid,bhjd->bhij', qc, kc) * jnp.exp(log_d - m_row[..., None])
        w_inter = jnp.exp(log_inter - m_row)
        num = (jnp.einsum('bhij,bhje->bhie', s, vc)
               + w_inter[..., None] * jnp.einsum('bhid,bhde->bhie', qc, c_st))
        den = jnp.sum(s, -1) + w_inter * jnp.einsum('bhid,bhd->bhi', qc, n_st)
        h = num / jnp.maximum(jnp.abs(den), jnp.exp(-m_row))[..., None]
        b_last = bcum[..., -1]
        log_w = b_last[..., None] - bcum + ic
        m_new = jnp.maximum(b_last + m_st, jnp.max(log_w, -1))
        keep = jnp.exp(b_last + m_st - m_new)
        kw = kc * jnp.exp(log_w - m_new[..., None])[..., None]
        c_st = keep[..., None, None] * c_st + jnp.einsum('bhjd,bhje->bhde', kw, vc)
        n_st = keep[..., None] * n_st + jnp.sum(kw, axis=2)
        return (c_st, n_st, m_new), h

    b = q.shape[0]
    carry0 = (jnp.zeros((b, MLSTM_HEADS, HEAD_DIM, HEAD_DIM), jnp.float32),
              jnp.zeros((b, MLSTM_HEADS, HEAD_DIM), jnp.float32),
              jnp.zeros((b, MLSTM_HEADS), jnp.float32))
    _, h = lax.scan(step, carry0, (to_chunks(q), to_chunks(k), to_chunks(v), to_chunks(log_i), to_chunks(log_f)))
    return from_chunks(h)


def s5_ssm(u, lam_re, lam_im, log_dt, b_re, b_im, c_re, c_im, d_skip):
    f32 = jnp.float32
    bsz, l, _ = u.shape
    lam_re, lam_im = lam_re.astype(f32), lam_im.astype(f32)
    b_re, b_im, c_re, c_im = b_re.astype(f32), b_im.astype(f32), c_re.astype(f32), c_im.astype(f32)
    ug = u.reshape(bsz, l, S5_GROUPS, S5_GROUP)
    dt = jnp.exp(log_dt.astype(f32))[:, None]
    mag = jnp.exp(lam_re * dt)
    ang = lam_im * dt
    lb_re, lb_im = mag * jnp.cos(ang), mag * jnp.sin(ang)
    zr, zi = lb_re - 1.0, lb_im
    den = jnp.square(lam_re) + jnp.square(lam_im)
    w_re = (zr * lam_re + zi * lam_im) / den
    w_im = (zi * lam_re - zr * lam_im) / den
    bb_re = w_re[..., None] * b_re - w_im[..., None] * b_im
    bb_im = w_re[..., None] * b_im + w_im[..., None] * b_re
    bu_re = jnp.einsum('gph,blgh->blgp', bb_re, ug)
    bu_im = jnp.einsum('gph,blgh->blgp', bb_im, ug)
    a_re = jnp.broadcast_to(lb_re, bu_re.shape)
    a_im = jnp.broadcast_to(lb_im, bu_im.shape)

    def combine(e1, e2):
        a1r, a1i, b1r, b1i = e1
        a2r, a2i, b2r, b2i = e2
        return (a2r * a1r - a2i * a1i, a2r * a1i + a2i * a1r,
                a2r * b1r - a2i * b1i + b2r, a2r * b1i + a2i * b1r + b2i)

    _, _, x_re, x_im = lax.associative_scan(combine, (a_re, a_im, bu_re, bu_im), axis=1)
    y = jnp.einsum('ghp,blgp->blgh', c_re, x_re) - jnp.einsum('ghp,blgp->blgh', c_im, x_im)
    return y.reshape(bsz, l, S5_WIDTH) + d_skip.astype(f32) * u


def causal_conv(x, w, b):
    l = x.shape[1]
    xp = jnp.pad(x, ((0, 0), (CONV_WIDTH - 1, 0), (0, 0)))
    out = b.astype(jnp.float32)
    for tap in range(CONV_WIDTH):
        out = out + xp[:, tap:tap + l] * w[tap].astype(jnp.float32)
    return out


def hybrid_layer(x, cos, sin, p_l, w_in, conv_w, conv_b, i_bias, f_bias, ret_norm_w, mlstm_norm_w,
                 lam_re, lam_im, log_dt, b_re, b_im, c_re, c_im, s5_d, glu_w, glu_b,
                 w_out, ln1_w, ln1_b, w_up, w_down, w_gate, w_ple, ln2_w, ln2_b):
    f32 = jnp.float32
    bsz, l, _ = x.shape
    proj = jnp.matmul(x, w_in)
    rq, rk, rv, rg, mq, mk, mv, mo, mi, mf, su = jnp.split(proj, IN_SPLITS, axis=-1)

    def heads(t, h):
        return t.astype(f32).reshape(bsz, l, h, HEAD_DIM)

    q_r = apply_rotary(heads(rq, RET_HEADS), cos, sin)
    k_r = apply_rotary(heads(rk, RET_HEADS), cos, sin) * HEAD_DIM ** -0.5
    y_ret = head_norm(retention(q_r, k_r, heads(rv, RET_HEADS)), ret_norm_w) * jax.nn.silu(rg.astype(f32))

    qk = jax.nn.silu(causal_conv(jnp.concatenate([mq, mk], -1).astype(f32), conv_w, conv_b))
    q_m, k_m = jnp.split(qk, 2, axis=-1)
    log_i = mi.astype(f32) + i_bias.astype(f32)
    log_f = jax.nn.log_sigmoid(mf.astype(f32) + f_bias.astype(f32))
    h_m = mlstm(heads(q_m, MLSTM_HEADS), heads(k_m, MLSTM_HEADS) * HEAD_DIM ** -0.5,
                heads(mv, MLSTM_HEADS), log_i, log_f)
    y_m = head_norm(h_m, mlstm_norm_w) * jax.nn.sigmoid(mo.astype(f32))

    y_s = jax.nn.gelu(s5_ssm(su.astype(f32), lam_re, lam_im, log_dt, b_re, b_im, c_re, c_im, s5_d))
    y_s = y_s * jax.nn.sigmoid(jnp.matmul(y_s, glu_w) + glu_b)

    mix = jnp.matmul(jnp.concatenate([y_ret, y_m, y_s], -1), w_out)
    x = layer_norm(ALPHA * x + mix, ln1_w, ln1_b)

    ff = jnp.matmul(jnp.square(jax.nn.relu(jnp.matmul(x, w_up))), w_down)
    r = ALPHA * x + ff
    ple = jax.nn.sigmoid(jnp.matmul(r, w_gate)) * jnp.matmul(p_l, w_ple)
    return layer_norm(r + ple, ln2_w, ln2_b)


def setup_inputs(seed: int = 0) -> dict:
    key = jax.random.key(seed)
    ks = jax.random.split(key, 32)
    f32 = jnp.float32

    def nrm(k, shape, scale):
        return jax.random.normal(k, shape, f32) * scale

    col_scale = np.ones(IN_WIDTH, np.float32)
    col_scale[2 * RET_WIDTH:3 * RET_WIDTH] = BETA
    col_scale[4 * RET_WIDTH + 2 * MLSTM_WIDTH:4 * RET_WIDTH + 3 * MLSTM_WIDTH] = BETA
    positions = jnp.broadcast_to(jnp.arange(SEQ, dtype=jnp.int32), (BATCH, SEQ))
    return {
        'x': nrm(ks[0], (BATCH, SEQ, D_MODEL), 1.0),
        'p': nrm(ks[1], (DEPTH, BATCH, SEQ, PLE_DIM), 1.0),
        'positions': positions,
        'w_in': nrm(ks[2], (DEPTH, D_MODEL, IN_WIDTH), D_MODEL ** -0.5) * jnp.asarray(col_scale),
        'mlstm_conv_w': nrm(ks[3], (DEPTH, CONV_WIDTH, 2 * MLSTM_WIDTH), CONV_WIDTH ** -0.5),
        'mlstm_conv_b': nrm(ks[4], (DEPTH, 2 * MLSTM_WIDTH), 0.02),
        'mlstm_i_bias': nrm(ks[5], (DEPTH, MLSTM_HEADS), 0.1),
        'mlstm_f_bias': jnp.linspace(3.0, 6.0, MLSTM_HEADS, dtype=f32) + nrm(ks[6], (DEPTH, MLSTM_HEADS), 0.1),
        'ret_norm_w': 1.0 + nrm(ks[7], (DEPTH, RET_WIDTH), 0.02),
        'mlstm_norm_w': 1.0 + nrm(ks[8], (DEPTH, MLSTM_WIDTH), 0.02),
        's5_lambda_re': -0.5 + nrm(ks[9], (DEPTH, S5_GROUPS, S5_STATE), 0.01),
        's5_lambda_im': jnp.pi * jnp.arange(S5_STATE, dtype=f32) + nrm(ks[10], (DEPTH, S5_GROUPS, S5_STATE), 0.01),
        's5_log_dt': jax.random.uniform(ks[11], (DEPTH, S5_GROUPS), f32, math.log(0.001), math.log(0.1)),
        's5_B_re': nrm(ks[12], (DEPTH, S5_GROUPS, S5_STATE, S5_GROUP), (2 * S5_GROUP) ** -0.5),
        's5_B_im': nrm(ks[13], (DEPTH, S5_GROUPS, S5_STATE, S5_GROUP), (2 * S5_GROUP) ** -0.5),
        's5_C_re': nrm(ks[14], (DEPTH, S5_GROUPS, S5_GROUP, S5_STATE), (2 * S5_STATE) ** -0.5),
        's5_C_im': nrm(ks[15], (DEPTH, S5_GROUPS, S5_GROUP, S5_STATE), (2 * S5_STATE) ** -0.5),
        's5_D': nrm(ks[16], (DEPTH, S5_WIDTH), 1.0),
        's5_glu_w': nrm(ks[17], (DEPTH, S5_WIDTH, S5_WIDTH), S5_WIDTH ** -0.5),
        's5_glu_b': nrm(ks[18], (DEPTH, S5_WIDTH), 0.02),
        'w_out': nrm(ks[19], (DEPTH, D_MODEL, D_MODEL), BETA * D_MODEL ** -0.5),
        'ln1_w': 1.0 + nrm(ks[20], (DEPTH, D_MODEL), 0.02),
        'ln1_b': nrm(ks[21], (DEPTH, D_MODEL), 0.02),
        'w_up': nrm(ks[22], (DEPTH, D_MODEL, D_FF), D_MODEL ** -0.5),
        'w_down': nrm(ks[23], (DEPTH, D_FF, D_MODEL), BETA * D_FF ** -0.5),
        'w_gate': nrm(ks[24], (DEPTH, D_MODEL, D_MODEL), D_MODEL ** -0.5),
        'w_ple': nrm(ks[25], (DEPTH, PLE_DIM, D_MODEL), BETA * PLE_DIM ** -0.5),
        'ln2_w': 1.0 + nrm(ks[26], (DEPTH, D_MODEL), 0.02),
        'ln2_b': nrm(ks[27], (DEPTH, D_MODEL), 0.02),
    }


def reference(x, p, positions, w_in, mlstm_conv_w, mlstm_conv_b, mlstm_i_bias, mlstm_f_bias,
              ret_norm_w, mlstm_norm_w, s5_lambda_re, s5_lambda_im, s5_log_dt, s5_B_re, s5_B_im,
              s5_C_re, s5_C_im, s5_D, s5_glu_w, s5_glu_b, w_out, ln1_w, ln1_b, w_up, w_down,
              w_gate, w_ple, ln2_w, ln2_b):
    f32 = jnp.float32
    inv_freq = ROPE_BASE ** (-jnp.arange(0, HEAD_DIM, 2, dtype=f32) / HEAD_DIM)
    ang = positions.astype(f32)[..., None] * inv_freq
    cos = jnp.cos(ang)[:, :, None, :]
    sin = jnp.sin(ang)[:, :, None, :]
    for i in range(DEPTH):
        x = hybrid_layer(x, cos, sin, p[i], w_in[i], mlstm_conv_w[i], mlstm_conv_b[i], mlstm_i_bias[i],
                         mlstm_f_bias[i], ret_norm_w[i], mlstm_norm_w[i], s5_lambda_re[i], s5_lambda_im[i],
                         s5_log_dt[i], s5_B_re[i], s5_B_im[i], s5_C_re[i], s5_C_im[i], s5_D[i],
                         s5_glu_w[i], s5_glu_b[i], w_out[i], ln1_w[i], ln1_b[i], w_up[i], w_down[i],
                         w_gate[i], w_ple[i], ln2_w[i], ln2_b[i])
    return x
```

```python
import math
from contextlib import ExitStack
import numpy as np
import concourse.bass as bass
import concourse.mybir as mybir
from concourse.bass_utils import run_bass_kernel_spmd

F32 = mybir.dt.float32
BF16 = mybir.dt.bfloat16
I32 = mybir.dt.int32
AF = mybir.ActivationFunctionType
ALU = mybir.AluOpType

NCORES = 8
SEQ = 16384
D = 2048
DFF = 8192
PLE = 256
INW = 6668
HD = 128
NH = 12
VW = 130
ALPHA = (2 * 2) ** 0.25
EPS = 1e-5
C_RQ, C_RK, C_RV, C_RG, C_MQ, C_MK, C_MV, C_MO, C_MI, C_MF, C_SU = 0, 768, 1536, 2304, 3072, 3840, 4608, 5376, 6144, 6150, 6156
TWO_PI = 2.0 * math.pi
CW1 = 6.28125
CW2 = TWO_PI - 6.28125


class Sched:
    ENGS = ("pe", "act", "dve", "pool", "sp")

    def __init__(self, nc, same_engine_sync=True):
        self.nc = nc
        self.lists = {e: [] for e in self.ENGS}
        self.cnt = {}
        self.seen = {e: {} for e in self.ENGS}
        self.last_write = {}
        self.readers = {}
        self.sem_names = []
        self.same_engine_sync = same_engine_sync
        self.out_tokens = []
        for e in self.ENGS:
            self._sem("E_" + e)

    def _sem(self, name):
        if name not in self.cnt:
            self.cnt[name] = 0
            self.sem_names.append(name)
        return name

    def _deps(self, e, reads, writes, own_dma_sem=None):
        toks = {}

        def add(t):
            if t is None:
                return
            s, v = t
            if toks.get(s, 0) < v:
                toks[s] = v
        for b in reads:
            add(self.last_write.get(b))
        for b in writes:
            lw = self.last_write.get(b)
            if lw is not None and not (own_dma_sem is not None and lw[0] == own_dma_sem):
                add(lw)
            for t in self.readers.get(b, ()):
                add(t)
        own = "E_" + e
        for s, v in toks.items():
            if s == own and (e == "pe" or not self.same_engine_sync):
                continue
            if self.seen[e].get(s, 0) < v:
                self.seen[e][s] = v
                self.lists[e].append(("wait", s, v))

    def _commit(self, tok, reads, writes):
        for b in writes:
            self.last_write[b] = tok
            self.readers[b] = []
        for b in reads:
            self.readers.setdefault(b, []).append(tok)

    def op(self, e, fn, reads=(), writes=()):
        self._deps(e, reads, writes)
        s = "E_" + e
        self.cnt[s] += 1
        tok = (s, self.cnt[s])
        self.lists[e].append(("op", fn, s, 1))
        self._commit(tok, reads, writes)
        return tok

    def dma(self, q, out, in_, slot, reads=(), writes=(), is_output=False):
        s = self._sem("D_" + str(slot))
        self._deps(q, reads, writes, own_dma_sem=s)
        self.cnt[s] += 16
        tok = (s, self.cnt[s])
        self.lists[q].append(("op", lambda eng, o=out, i=in_: eng.dma_start(out=o, in_=i), s, 16))
        self._commit(tok, reads, writes)
        if is_output:
            self.out_tokens.append(tok)
        return tok

    def barrier(self):
        for e in self.ENGS:
            for s in self.sem_names:
                v = self.cnt[s]
                if v > 0 and self.seen[e].get(s, 0) < v:
                    self.seen[e][s] = v
                    self.lists[e].append(("wait", s, v))

    def finish(self):
        self.barrier()

    def emit(self):
        nc = self.nc
        with ExitStack() as st:
            sems = {}
            for i, name in enumerate(self.sem_names):
                sems[name] = st.enter_context(nc.semaphore("s%d" % i))
            block = st.enter_context(nc.Block())
            lists = self.lists

            def replay(eng, items):
                for it in items:
                    if it[0] == "wait":
                        eng.wait_ge(sems[it[1]], it[2])
                    else:
                        it[1](eng).then_inc(sems[it[2]], it[3])

            @block.tensor
            def _(eng):
                replay(eng, lists["pe"])

            @block.scalar
            def _(eng):
                replay(eng, lists["act"])

            @block.vector
            def _(eng):
                replay(eng, lists["dve"])

            @block.gpsimd
            def _(eng):
                replay(eng, lists["pool"])

            @block.sync
            def _(eng):
                replay(eng, lists["sp"])


class Arena:
    def __init__(self, tile, size):
        self.tile = tile
        self.size = size
        self.off = 0

    def reset(self):
        self.off = 0

    def f32(self, n):
        assert self.off + n <= self.size, ("arena overflow", self.off, n, self.size)
        ap = self.tile[:, self.off:self.off + n]
        self.off += n
        return ap

    def bf16(self, n):
        assert n % 2 == 0
        return self.f32(n // 2).bitcast(BF16)


def build(NT, NL=1, stage="B", dbg=()):
    T = NT * 128
    TB = min(512, T)
    NTB = T // TB
    TPB = TB // 128
    nc = bass.Bass("TRN2", target_bir_lowering=False)
    S = Sched(nc)
    DT = lambda name, shape, dt, kind="Internal": nc.dram_tensor(name, shape, dt, kind=kind).ap()
    IN = lambda name, shape, dt=F32: DT(name, shape, dt, "ExternalInput")
    OUT = lambda name, shape, dt=F32: DT(name, shape, dt, "ExternalOutput")

    x_in = IN("x", [T, D])
    xh_in = IN("xh", [4, D])
    pos_in = IN("pos", [128, NT], I32)
    cmask_in = IN("cmask", [128, NCORES])
    consts_in = IN("consts", [128, 128 * 4 + 64 + 18])
    A_SET = ("w_in", "cw", "cb", "gbias", "s5p", "s5rep", "s5B")
    SHAPES = dict(
        p=[T, PLE], w_in=[D, INW], cw=[128, 12, 4], cb=[128, 12], gbias=[128, 12], hnw=[128, NH * 128],
        s5p=[128, 3, 16], s5rep=[128, 3, 2048], s5B=[2, 128, 16, 128], s5C=[2, 128, 16, 128], s5D=[128, 4],
        glu_w=[512, 512], glu_b=[128, 4], w_out=[D, D], lnp=[4, 128, D], w_up=[D, DFF], w_down=[DFF, D],
        w_gate=[D, D], w_ple=[PLE, D], g_S=[NCORES, 128, NH * VW], g_A=[NCORES, 128, NH], g_X=[NCORES, 128, 32])
    L = []
    for l in range(NL):
        dd = {}
        for nm, shp in SHAPES.items():
            if stage == "A" and nm not in A_SET:
                continue
            if stage == "F" and nm.startswith("g_"):
                continue
            dd[nm] = IN(nm + str(l), shp)
        L.append(dd)
    outs = {}
    if stage == "A":
        outs["sum_S"] = OUT("sum_S", [128, NH * VW])
        outs["sum_A"] = OUT("sum_A", [128, NH])
        outs["sum_X"] = OUT("sum_X", [128, 32])
    else:
        outs["y"] = OUT("y", [T, D])
    dbg_out = {}
    SUMW = NH * VW + NH + 32
    if stage == "F":
        oh_in = IN("oh", [128, NCORES])
        ccsrc = [DT("ccsrc%d" % l, [128, SUMW], F32) for l in range(NL)]
        ccdst = [DT("ccdst%d" % l, [NCORES * 128, SUMW], F32) for l in range(NL)]
        hsrc = DT("hsrc", [4, D], F32)
        hdst = DT("hdst", [NCORES * 4, D], F32)
        halo1 = DT("halo1", [4, D], F32)

    def gsrc(l, nm, j):
        if stage == "F":
            v = ccdst[l].rearrange("(r p) n -> r p n", p=128)[j]
            if nm == "g_S":
                return v[:, 0:NH * VW]
            if nm == "g_A":
                return v[:, NH * VW:NH * VW + NH]
            return v[:, NH * VW + NH:SUMW]
        return L[l][nm][j]

    QK = DT("QK", [NT, 128, 2, NH, 128], BF16)
    VV = DT("VV", [NT, 128, NH, VW], BF16)
    GT = DT("GT", [NT, 128, NH * 128], F32)
    UT = DT("UT", [4, 128, T], F32)
    R1 = DT("R1", [NT, 128, D], F32)
    X1 = DT("X1", [NT, 128, D], F32)
    HH = DT("HH", [NT, 128, 64, 128], BF16)
    R2 = DT("R2", [NT, 128, D], F32)
    R3 = DT("R3", [NT, 128, D], F32)
    XB = DT("XB", [NT, 128, D], F32)

    with ExitStack() as st:
        TL = lambda name, shape, dt: st.enter_context(nc.sbuf_tensor(name, shape, dt))
        PS = lambda name, shape, dt: st.enter_context(nc.psum_tensor(name, shape, dt))
        AT = TL("AT", [128, 16, T], BF16)
        ATP = TL("ATP", [128, 2, T], BF16)
        ATH = TL("ATH", [128, 16, 4], BF16)
        CST = TL("CST", [128, 128 * 4 + 64 + 18], F32)
        identb = TL("identb", [128, 128], BF16)
        maskb = TL("maskb", [128, 128], BF16)
        QS = TL("QS", [128, NT, NH], F32)
        KS = TL("KS", [128, NT, NH], F32)
        AA = TL("AA", [128, NT, NH], F32)
        COS = TL("COS", [128, NT, 64], F32)
        SIN = TL("SIN", [128, NT, 64], F32)
        POSI = TL("POSI", [128, NT], I32)
        POSF = TL("POSF", [128, NT], F32)
        CMASK = TL("CMASK", [128, NCORES], F32)
        SMALL = TL("SMALL", [128, 1024], F32)
        ARSZ = 29184
        ARENA = TL("ARENA", [128, ARSZ], F32)
        ar = Arena(ARENA, ARSZ)
        ps = [PS("ps%d" % i, [128, 512], F32) for i in range(7)]
        psb = PS("psb", [128, 1024], BF16)

        identf = CST[:, 0:128]
        tri = CST[:, 128:256]
        onesf = CST[:, 256:384]
        invf = CST[:, 512:576]
        retc = CST[:, 576:594]

        sm_off = [0]

        def sm(n):
            ap = SMALL[:, sm_off[0]:sm_off[0] + n]
            sm_off[0] += n
            assert sm_off[0] <= 1024
            return ap

        def OP(e, method, *args, reads=(), writes=(), **kw):
            return S.op(e, lambda eng, m=method, a=args, k=kw: getattr(eng, m)(*a, **k), reads, writes)

        S.dma("sp", CST[:], consts_in, "CST", writes=["CST"])
        S.dma("sp", POSI[:], pos_in, "POSI", writes=["POSI"])
        S.dma("sp", CMASK[:], cmask_in, "CMASK", writes=["CMASK"])
        OP("dve", "tensor_copy", identb[:], identf, reads=["CST"], writes=["identb"])
        OP("dve", "tensor_copy", maskb[:], tri, reads=["CST"], writes=["maskb"])
        OP("dve", "tensor_copy", POSF[:], POSI[:], reads=["POSI"], writes=["POSF"])

        def range_reduce_sin(dst, ang, tmp_i, tmp_f, keys, n):
            OP("dve", "tensor_scalar", tmp_f, ang, 1.0 / TWO_PI, None, ALU.mult, reads=keys, writes=keys)
            OP("dve", "tensor_copy", tmp_i, tmp_f, reads=keys, writes=keys)
            OP("dve", "tensor_copy", tmp_f, tmp_i, reads=keys, writes=keys)
            OP("dve", "scalar_tensor_tensor", ang, tmp_f, -CW1, ang, ALU.mult, ALU.add, reads=keys, writes=keys)
            OP("dve", "scalar_tensor_tensor", ang, tmp_f, -CW2, ang, ALU.mult, ALU.add, reads=keys, writes=keys)
            OP("dve", "tensor_scalar", ang, ang, math.pi, -math.pi, ALU.min, ALU.max, reads=keys, writes=keys)
            OP("act", "activation", dst, ang, AF.Sin, reads=keys, writes=keys)

        def cos_from_reduced(dst, red, tmp, tmp2, keys):
            OP("dve", "tensor_scalar", tmp, red, math.pi / 2, None, ALU.add, reads=keys, writes=keys)
            OP("dve", "tensor_scalar", tmp2, tmp, math.pi, -TWO_PI, ALU.is_gt, ALU.mult, reads=keys, writes=keys)
            OP("dve", "tensor_tensor", tmp, tmp, tmp2, ALU.add, reads=keys, writes=keys)
            OP("dve", "tensor_scalar", tmp, tmp, math.pi, -math.pi, ALU.min, ALU.max, reads=keys, writes=keys)
            OP("act", "activation", dst, tmp, AF.Sin, reads=keys, writes=keys)

        ar.reset()
        ang = ar.f32(NT * 64)
        ang2 = ar.f32(NT * 64)
        tf = ar.f32(NT * 64)
        ti = ar.f32(NT * 64).bitcast(I32)
        kk = ["rot"]
        for tt in range(NT):
            OP("dve", "tensor_scalar", ang[:, tt * 64:(tt + 1) * 64], invf, POSF[:, tt:tt + 1], None, ALU.mult,
               reads=["CST", "POSF"] + kk, writes=kk)
        range_reduce_sin(SIN[:].rearrange("p a b -> p (a b)"), ang, ti, tf, kk + ["SIN"], NT * 64)
        cos_from_reduced(COS[:].rearrange("p a b -> p (a b)"), ang, ang2, tf, kk + ["COS"])

        if "rot" in dbg:
            dbg_out["d_cos"] = OUT("d_cos", [128, NT, 64])
            dbg_out["d_sin"] = OUT("d_sin", [128, NT, 64])
            S.dma("sp", dbg_out["d_cos"], COS[:], "dbgc", reads=["COS"], is_output=True)
            S.dma("sp", dbg_out["d_sin"], SIN[:], "dbgs", reads=["SIN"], is_output=True)

        def load_AT_from(src_tiles, src_halo=None):
            S.barrier()
            ar.reset()
            xs = [ar.f32(D), ar.f32(D)]
            for tt in range(NT):
                sl = tt % 2
                S.dma("sp", xs[sl], src_tiles(tt), "xs%d" % sl, writes=["xs%d" % sl])
                for g in range(4):
                    b = ps[g]
                    for j in range(4):
                        kc = 4 * g + j
                        OP("pe", "transpose", b[:, j * 128:(j + 1) * 128], xs[sl][:, kc * 128:(kc + 1) * 128], identf,
                           reads=["xs%d" % sl, "CST"], writes=["ps%d" % g])
                    eng = "act" if g % 2 == 0 else "dve"
                    dst = AT[:, 4 * g:4 * g + 4, tt * 128:(tt + 1) * 128]
                    srcv = b[:].rearrange("p (a b) -> p a b", a=4)
                    if eng == "act":
                        OP("act", "activation", dst, srcv, AF.Copy, reads=["ps%d" % g], writes=["AT"])
                    else:
                        OP("dve", "tensor_copy", dst, srcv, reads=["ps%d" % g], writes=["AT"])
            if src_halo is not None:
                xh = ar.f32(D)
                S.dma("sp", xh[0:4, :], src_halo, "xh", writes=["xh"])
                for kc in range(16):
                    OP("pe", "transpose", ps[4][:, kc * 4:(kc + 1) * 4], xh[0:4, kc * 128:(kc + 1) * 128], identf[0:4, 0:4],
                       reads=["xh", "CST"], writes=["ps4"])
                OP("dve", "tensor_copy", ATH[:], ps[4][:, 0:64].rearrange("p (a b) -> p a b", a=16), reads=["ps4"], writes=["ATH"])

        def load_w(wslots, gi, w_dram, c0, width, KC, key="wt"):
            sl = gi % 2
            dst = wslots[sl][:, 0:KC * width].rearrange("p (k n) -> p k n", k=KC)
            src = w_dram.rearrange("(k p) n -> p k n", p=128)[:, :, c0:c0 + width]
            nsp = 4 if KC >= 4 else 1
            step = KC // nsp
            for i in range(nsp):
                S.dma("pool", dst[:, i * step:(i + 1) * step, :], src[:, i * step:(i + 1) * step, :], "%s%d" % (key, sl),
                      writes=["%s%d" % (key, sl)])
            return dst, "%s%d" % (key, sl)

        def layer(l, x_tiles, x_halo, out_tiles, out_is_ext):
            P = L[l]
            load_AT_from(x_tiles, x_halo)
            S.barrier()
            ar.reset()
            sm_off[0] = 0
            wsl = [ar.bf16(16 * 512), ar.bf16(16 * 512)]
            gbias = sm(12)
            S.dma("sp", gbias, P["gbias"], "gbias", writes=["gbias"])
            cwt = sm(48)
            cbt = sm(12)
            S.dma("sp", cwt, P["cw"].rearrange("p a b -> p (a b)"), "cwt", writes=["cwt"])
            S.dma("sp", cbt, P["cb"], "cbt", writes=["cbt"])
            for (tab, c) in ((QS, 0), (KS, 6), (AA, 12)):
                OP("dve", "tensor_copy", tab[:, :, 0:6], retc[:, c:c + 6].unsqueeze(1).to_broadcast([128, NT, 6]),
                   reads=["CST"], writes=["tabs"])

            gi = 0
            wv, wkey = load_w(wsl, gi, P["w_in"], C_MI, 12, 16)
            gt = [sm(64), sm(64)]
            for tt in range(NT):
                pa = ps[tt % 2]
                pk = "ps%d" % (tt % 2)
                for kc in range(16):
                    OP("pe", "matmul", pa[:, 0:12], AT[:, kc, tt * 128:(tt + 1) * 128], wv[:, kc, :], start=(kc == 0), stop=(kc == 15),
                       reads=["AT", wkey], writes=[pk])
                g = gt[tt % 2]
                gk = "gt%d" % (tt % 2)
                tif = g[:, 0:12]
                ee = g[:, 12:18]
                spv = g[:, 18:24]
                tmp = g[:, 24:30]
                OP("dve", "tensor_tensor", tif, pa[:, 0:12], gbias, ALU.add, reads=[pk, "gbias"], writes=[gk])
                OP("act", "activation", ee, tif[:, 6:12], AF.Exp, scale=-1.0, reads=[gk], writes=[gk])
                OP("act", "activation", spv, ee, AF.Ln, bias=1.0, reads=[gk], writes=[gk])
                pb = ps[2 + tt % 2]
                pbk = "ps%d" % (2 + tt % 2)
                OP("pe", "matmul", pb[:, 0:6], tri, spv, start=True, stop=True, reads=["CST", gk], writes=[pbk])
                OP("pe", "matmul", pb[:, 6:12], onesf, spv, start=True, stop=True, reads=["CST", gk], writes=[pbk])
                OP("act", "activation", QS[:, tt, 6:12], pb[:, 0:6], AF.Exp, scale=-1.0, reads=[pbk], writes=["tabs"])
                OP("dve", "tensor_tensor", tmp, pb[:, 0:6], tif[:, 0:6], ALU.add, reads=[pbk, gk], writes=[gk])
                OP("act", "activation", KS[:, tt, 6:12], tmp, AF.Exp, bias=-0.5 * math.log(128.0), reads=[gk], writes=["tabs"])
                OP("act", "activation", AA[:, tt, 6:12], pb[:, 6:12], AF.Exp, scale=-1.0, reads=[pbk], writes=["tabs"])
            gi += 1

            rt = [ar.f32(4 * 192), ar.f32(4 * 192)]
            ro = [ar.f32(384), ar.f32(384)]
            qstg = [ar.bf16(384), ar.bf16(384)]
            vstg = [ar.bf16(3 * VW), ar.bf16(3 * VW)]
            gstg = [ar.f32(384), ar.f32(384)]
            for sl in range(2):
                OP("pool", "memset", vstg[sl].rearrange("p (h c) -> p h c", h=3)[:, :, 128:129], 1.0, writes=["vstg%d" % sl])
                OP("pool", "memset", vstg[sl].rearrange("p (h c) -> p h c", h=3)[:, :, 129:130], 0.0, writes=["vstg%d" % sl])
            tok_groups = []
            for half in range(2):
                tok_groups.append(("q", C_RQ + 384 * half, 3 * half))
            for half in range(2):
                tok_groups.append(("k", C_RK + 384 * half, 3 * half))
            for half in range(2):
                tok_groups.append(("v", C_RV + 384 * half, 3 * half))
            for half in range(2):
                tok_groups.append(("silu", C_RG + 384 * half, 3 * half))
            for half in range(2):
                tok_groups.append(("v", C_MV + 384 * half, 6 + 3 * half))
            for half in range(2):
                tok_groups.append(("sig", C_MO + 384 * half, 6 + 3 * half))
            it = 0
            for (kind, c0, h0) in tok_groups:
                wv, wkey = load_w(wsl, gi, P["w_in"], c0, 384, 16)
                gi += 1
                for tt in range(NT):
                    b = it % 4
                    sl = it % 2
                    it += 1
                    pa = ps[b]
                    pk = "ps%d" % b
                    for kc in range(16):
                        OP("pe", "matmul", pa[:, 0:384], AT[:, kc, tt * 128:(tt + 1) * 128], wv[:, kc, :], start=(kc == 0), stop=(kc == 15),
                           reads=["AT", wkey], writes=[pk])
                    if kind in ("q", "k"):
                        pv = pa[:, 0:384].rearrange("p (h t f) -> p h t f", h=3, t=2)
                        t1 = pv[:, :, 0, :]
                        t2 = pv[:, :, 1, :]
                        cosb = COS[:, tt, :].unsqueeze(1).to_broadcast([128, 3, 64])
                        sinb = SIN[:, tt, :].unsqueeze(1).to_broadcast([128, 3, 64])
                        r = rt[sl].rearrange("p (a h f) -> p a h f", a=4, h=3)
                        rk_ = "rt%d" % sl
                        OP("dve", "tensor_tensor", r[:, 0], t1, cosb, ALU.mult, reads=[pk, "COS"], writes=[rk_])
                        OP("dve", "tensor_tensor", r[:, 1], t2, sinb, ALU.mult, reads=[pk, "SIN"], writes=[rk_])
                        OP("dve", "tensor_tensor", r[:, 2], t1, sinb, ALU.mult, reads=[pk, "SIN"], writes=[rk_])
                        OP("dve", "tensor_tensor", r[:, 3], t2, cosb, ALU.mult, reads=[pk, "COS"], writes=[rk_])
                        rov = ro[sl].rearrange("p (h t f) -> p h t f", h=3, t=2)
                        rok = "ro%d" % sl
                        OP("pool", "tensor_tensor", rov[:, :, 0, :], r[:, 0], r[:, 1], ALU.subtract, reads=[rk_], writes=[rok])
                        OP("pool", "tensor_tensor", rov[:, :, 1, :], r[:, 2], r[:, 3], ALU.add, reads=[rk_], writes=[rok])
                        tab = QS if kind == "q" else KS
                        qk_ = "qstg%d" % sl
                        for hh in range(3):
                            OP("act", "activation", qstg[sl][:, hh * 128:(hh + 1) * 128], ro[sl][:, hh * 128:(hh + 1) * 128], AF.Copy,
                               scale=tab[:, tt, h0 + hh:h0 + hh + 1], reads=[rok, "tabs"], writes=[qk_])
                        which = 0 if kind == "q" else 1
                        S.dma("sp", QK[tt, :, which, h0:h0 + 3, :], qstg[sl].rearrange("p (h d) -> p h d", h=3), qk_, reads=[qk_],
                              writes=[("QK", tt)])
                    elif kind == "v":
                        vk = "vstg%d" % sl
                        OP("act", "activation", vstg[sl].rearrange("p (h c) -> p h c", h=3)[:, :, 0:128],
                           pa[:, 0:384].rearrange("p (h c) -> p h c", h=3), AF.Copy, reads=[pk], writes=[vk])
                        S.dma("sp", VV[tt, :, h0:h0 + 3, :], vstg[sl].rearrange("p (h c) -> p h c", h=3), vk, reads=[vk], writes=[("VV", tt)])
                    else:
                        gk = "gstg%d" % sl
                        OP("act", "activation", gstg[sl], pa[:, 0:384], AF.Silu if kind == "silu" else AF.Sigmoid, reads=[pk], writes=[gk])
                        S.dma("sp", GT[tt, :, h0 * 128:(h0 + 3) * 128], gstg[sl], gk, reads=[gk], writes=[("GT", tt)])

            cin = ar.f32(T + 8)
            cacc = ar.f32(T)
            csil = ar.f32(T)
            q2 = ar.bf16(NT * 128)
            ustg = [ar.f32(TB), ar.f32(TB)]
            feat_groups = [("mq", C_MQ, 384, 0), ("mq", C_MQ + 384, 384, 3), ("mk", C_MK, 384, 0), ("mk", C_MK + 384, 384, 3),
                           ("su", C_SU, 512, 0)]
            itf = 0
            for (kind, c0, width, f0) in feat_groups:
                wv, wkey = load_w(wsl, gi, P["w_in"], c0, width, 16)
                gi += 1
                for fi in range(width // 128):
                    for tb in range(NTB):
                        b = itf % 4
                        itf += 1
                        pa = ps[b]
                        pk = "ps%d" % b
                        for kc in range(16):
                            OP("pe", "matmul", pa[:, 0:TB], wv[:, kc, fi * 128:(fi + 1) * 128], AT[:, kc, tb * TB:(tb + 1) * TB],
                               start=(kc == 0), stop=(kc == 15), reads=["AT", wkey], writes=[pk])
                        if kind == "su":
                            sl = itf % 2
                            OP("act", "activation", ustg[sl], pa[:, 0:TB], AF.Copy, reads=[pk], writes=["ustg%d" % sl])
                            S.dma("sp", UT[fi, :, tb * TB:(tb + 1) * TB], ustg[sl], "ustg%d" % sl, reads=["ustg%d" % sl], writes=["UT"])
                        else:
                            OP("act", "activation", cin[:, 4 + tb * TB:4 + (tb + 1) * TB], pa[:, 0:TB], AF.Copy, reads=[pk], writes=["cin"])
                    if kind == "su":
                        continue
                    for kc in range(16):
                        OP("pe", "matmul", ps[4][:, 0:4], wv[:, kc, fi * 128:(fi + 1) * 128], ATH[:, kc, :], start=(kc == 0), stop=(kc == 15),
                           reads=["ATH", wkey], writes=["ps4"])
                    OP("act", "activation", cin[:, 0:4], ps[4][:, 0:4], AF.Copy, reads=["ps4"], writes=["cin"])
                    h = f0 + fi
                    fidx = h if kind == "mq" else 6 + h
                    OP("dve", "tensor_scalar", cacc, cin[:, 1:1 + T], cwt[:, fidx * 4:fidx * 4 + 1], cbt[:, fidx:fidx + 1], ALU.mult, ALU.add,
                       reads=["cin", "cwt", "cbt"], writes=["cacc"])
                    for tap in range(1, 4):
                        OP("dve", "scalar_tensor_tensor", cacc, cin[:, 1 + tap:1 + tap + T], cwt[:, fidx * 4 + tap:fidx * 4 + tap + 1], cacc,
                           ALU.mult, ALU.add, reads=["cin", "cwt", "cacc"], writes=["cacc"])
                    OP("act", "activation", csil, cacc, AF.Silu, reads=["cacc"], writes=["csil"])
                    tab = QS if kind == "mq" else KS
                    for tt in range(NT):
                        b = 5 + tt % 2
                        OP("pe", "transpose", ps[b][:, 0:128], csil[:, tt * 128:(tt + 1) * 128], identf, reads=["csil", "CST"], writes=["ps%d" % b])
                        OP("act", "activation", q2[:, tt * 128:(tt + 1) * 128], ps[b][:, 0:128], AF.Copy, scale=tab[:, tt, 6 + h:7 + h],
                           reads=["ps%d" % b, "tabs"], writes=["q2"])
                    which = 0 if kind == "mq" else 1
                    S.dma("sp", QK[:, :, which, 6 + h, :].rearrange("t p d -> p t d"), q2.rearrange("p (t d) -> p t d", t=NT), "q2",
                          reads=["q2"], writes=[("QK", tt) for tt in range(NT)])

            if "p1" in dbg:
                S.barrier()
                for nm, src, shp, dt_ in (("d_QK", QK, [NT, 128, 2, NH, 128], BF16), ("d_VV", VV, [NT, 128, NH, VW], BF16),
                                          ("d_GT", GT, [NT, 128, NH * 128], F32), ("d_UT", UT, [4, 128, T], F32)):
                    dbg_out[nm] = OUT(nm, shp, dt_)
                    S.dma("sp", dbg_out[nm], src, nm, is_output=True)
                tabs_o = OUT("d_tabs", [128, 3, NT, NH])
                dbg_out["d_tabs"] = tabs_o
                for i, tb_ in enumerate((QS, KS, AA)):
                    S.dma("sp", tabs_o[:, i], tb_[:], "dtab%d" % i, reads=["tabs"], is_output=True)
                return


            def pm(full):
                CUT = 99
                S.barrier()
                ar.reset()
                sm_off[0] = 0
                Sst = ar.f32(NH * VW)
                Sbf = ar.bf16(NH * VW)
                Sv = Sst.rearrange("p (h c) -> p h c", h=NH)
                Sbv = Sbf.rearrange("p (h c) -> p h c", h=NH)
                qksl = [ar.bf16(2 * NH * 128), ar.bf16(2 * NH * 128)]
                vsl = [ar.bf16(NH * VW), ar.bf16(NH * VW)]
                OP("dve", "memset", Sst, 0.0, writes=["Sst"])
                if full:
                    gS = [ar.f32(NH * VW), ar.f32(NH * VW)]
                    gA = [sm(NH), sm(NH)]
                    aeff = sm(NH)
                    for j in range(NCORES):
                        sl = j % 2
                        S.dma("sp", gS[sl], gsrc(l, "g_S", j), "gS%d" % sl, reads=[("ccdst", l)], writes=["gS%d" % sl])
                        S.dma("sp", gA[sl], gsrc(l, "g_A", j), "gA%d" % sl, reads=[("ccdst", l)], writes=["gA%d" % sl])
                        OP("dve", "tensor_scalar", aeff, gA[sl], -1.0, CMASK[:, j:j + 1], ALU.add, ALU.mult, reads=["gA%d" % sl, "CMASK"], writes=["aeff"])
                        OP("dve", "tensor_scalar", aeff, aeff, 1.0, None, ALU.add, reads=["aeff"], writes=["aeff"])
                        OP("dve", "tensor_scalar", gS[sl], gS[sl], CMASK[:, j:j + 1], None, ALU.mult, reads=["gS%d" % sl, "CMASK"], writes=["gS%d" % sl])
                        gv = gS[sl].rearrange("p (h c) -> p h c", h=NH)
                        for h in range(NH):
                            OP("dve", "scalar_tensor_tensor", Sv[:, h, :], Sv[:, h, :], aeff[:, h:h + 1], gv[:, h, :], ALU.mult, ALU.add,
                               reads=["Sst", "aeff", "gS%d" % sl], writes=["Sst"])
                OP("dve", "tensor_copy", Sbf, Sst, reads=["Sst"], writes=["Sbf"])
                atot = sm(NH)
                OP("dve", "memset", atot, 1.0, writes=["atot"])
                if full:
                    gsl = [ar.f32(NH * 128), ar.f32(NH * 128)]
                    hnw = ar.f32(NH * 128)
                    S.dma("sp", hnw, P["hnw"], "hnw", writes=["hnw"])
                    qkT = ar.bf16(24 * 128)
                    qkTv = qkT.rearrange("p (a d) -> p a d", a=24)
                    sT = ar.bf16(6 * 128)
                    sTv = sT.rearrange("p (h d) -> p h d", h=6)
                    yn = ar.f32(6 * 128)
                    ybf = ar.bf16(6 * 128)
                    gw = ar.f32(NH * 128)
                    st6 = [sm(36), sm(36)]
                    mv2 = [sm(12), sm(12)]
                    sc6 = [sm(24), sm(24)]
                    ps6b = ps[6][:].bitcast(BF16)

                def load_chunk(tt):
                    sl = tt % 2
                    S.dma("sp", qksl[sl], QK[tt].rearrange("p a h d -> p (a h d)"), "qksl%d" % sl, reads=[("QK", tt)], writes=["qksl%d" % sl])
                    S.dma("sp", vsl[sl], VV[tt].rearrange("p h c -> p (h c)"), "vsl%d" % sl, reads=[("VV", tt)], writes=["vsl%d" % sl])
                    if full:
                        S.dma("sp", gsl[sl], GT[tt], "gsl%d" % sl, reads=[("GT", tt)], writes=["gsl%d" % sl])

                load_chunk(0)
                for tt in range(NT):
                    sl = tt % 2
                    if tt + 1 < NT:
                        load_chunk(tt + 1)
                    qkv = qksl[sl].rearrange("p (a h d) -> p a h d", a=2, h=NH)
                    vv = vsl[sl].rearrange("p (h c) -> p h c", h=NH)
                    qkk, vk, gk_ = "qksl%d" % sl, "vsl%d" % sl, "gsl%d" % sl
                    if full:
                        for r in range(3):
                            for j in range(8):
                                idx = r * 8 + j
                                OP("pe", "transpose", psb[:, j * 128:(j + 1) * 128], qkv[:, idx // NH, idx % NH, :], identb[:],
                                   reads=[qkk, "identb"], writes=["psb"])
                            if r % 2 == 0:
                                OP("act", "activation", qkT[:, r * 1024:(r + 1) * 1024], psb[:], AF.Copy, reads=["psb"], writes=["qkT"])
                            else:
                                OP("dve", "tensor_copy", qkT[:, r * 1024:(r + 1) * 1024], psb[:], reads=["psb"], writes=["qkT"])
                        OP("pool", "tensor_tensor", gw, gsl[sl], hnw, ALU.mult, reads=[gk_, "hnw"], writes=["gw"])
                    for half in range(2):
                        h0 = 6 * half
                        if full and CUT >= 2:
                            for hh in range(6):
                                h = h0 + hh
                                bank = ps[0] if hh < 4 else ps[1]
                                bk = "ps0" if hh < 4 else "ps1"
                                col = (hh % 4) * 128
                                OP("pe", "matmul", bank[:, col:col + 128], qkTv[:, NH + h, :], qkTv[:, h, :], start=True, stop=True,
                                   reads=["qkT"], writes=[bk])
                            OP("dve", "tensor_tensor", sTv[:, 0:4, :], ps[0][:].rearrange("p (h d) -> p h d", h=4),
                               maskb[:].unsqueeze(1).to_broadcast([128, 4, 128]), ALU.mult, reads=["ps0", "maskb"], writes=["sT"])
                            OP("dve", "tensor_tensor", sTv[:, 4:6, :], ps[1][:, 0:256].rearrange("p (h d) -> p h d", h=2),
                               maskb[:].unsqueeze(1).to_broadcast([128, 2, 128]), ALU.mult, reads=["ps1", "maskb"], writes=["sT"])
                            for hh in range(6 if CUT >= 3 else 0):
                                h = h0 + hh
                                bank = ps[2 + hh // 3]
                                bk = "ps%d" % (2 + hh // 3)
                                col = (hh % 3) * VW
                                OP("pe", "matmul", bank[:, col:col + VW], sTv[:, hh, :], vv[:, h, :], start=True, stop=False,
                                   reads=["sT", vk], writes=[bk])
                                OP("pe", "matmul", bank[:, col:col + VW], qkTv[:, h, :], Sbv[:, h, :], start=False, stop=True,
                                   reads=["qkT", "Sbf"], writes=[bk])
                        for hh in range(6):
                            h = h0 + hh
                            bank = ps[4 + hh // 3]
                            bk = "ps%d" % (4 + hh // 3)
                            col = (hh % 3) * VW
                            OP("pe", "matmul", bank[:, col:col + VW], qkv[:, 1, h, :], vv[:, h, :], start=True, stop=True,
                               reads=[qkk, vk], writes=[bk])
                        for g3 in range(2):
                            hs = h0 + 3 * g3
                            OP("dve", "tensor_tensor", Sv[:, hs:hs + 3, :], ps[4 + g3][:, 0:3 * VW].rearrange("p (h c) -> p h c", h=3), Sv[:, hs:hs + 3, :],
                               ALU.add, reads=["ps%d" % (4 + g3), "Sst"], writes=["Sst"])
                        for hh in range(6):
                            h = h0 + hh
                            OP("act", "activation", Sv[:, h, :], Sv[:, h, :], AF.Copy, scale=AA[:, tt, h:h + 1], reads=["Sst", "tabs"], writes=["Sst"])
                        OP("pool", "tensor_copy", Sbv[:, h0:h0 + 6, :], Sv[:, h0:h0 + 6, :], reads=["Sst"], writes=["Sbf"])
                        if not full or CUT < 5:
                            continue
                        s6 = st6[half]
                        m2 = mv2[half]
                        sc = sc6[half]
                        ek = "ep%d" % half
                        for hh in range(6):
                            bank = ps[2 + hh // 3]
                            col = (hh % 3) * VW
                            OP("dve", "bn_stats", s6[:, hh * 6:(hh + 1) * 6], bank[:, col:col + 128], reads=["ps%d" % (2 + hh // 3)], writes=[ek])
                            OP("dve", "bn_aggr", m2[:, hh * 2:(hh + 1) * 2], s6[:, hh * 6:(hh + 1) * 6], reads=[ek], writes=[ek])
                        if CUT < 6:
                            continue
                        m2v = m2.rearrange("p (h c) -> p h c", c=2)
                        mean = m2v[:, :, 0]
                        var = m2v[:, :, 1]
                        rstd = sc[:, 0:6]
                        nb = sc[:, 6:12]
                        rden = sc[:, 12:18]
                        tmp6 = sc[:, 18:24]
                        if half == 1:
                            for g3 in range(2):
                                OP("act", "activation", rden[:, 3 * g3:3 * g3 + 3], ps[2 + g3][:, 0:3 * VW].rearrange("p (h c) -> p h c", h=3)[:, :, 128],
                                   AF.Abs, reads=["ps%d" % (2 + g3)], writes=[ek])
                            OP("dve", "tensor_scalar", rden, rden, 1.0, None, ALU.max, reads=[ek], writes=[ek])
                            OP("dve", "reciprocal", rden, rden, reads=[ek], writes=[ek])
                            OP("dve", "tensor_tensor", tmp6, rden, rden, ALU.mult, reads=[ek], writes=[ek])
                            OP("dve", "tensor_tensor", tmp6, tmp6, var, ALU.mult, reads=[ek], writes=[ek])
                            OP("act", "activation", rstd, tmp6, AF.Sqrt, bias=EPS, reads=[ek], writes=[ek])
                            OP("dve", "reciprocal", rstd, rstd, reads=[ek], writes=[ek])
                            OP("dve", "tensor_tensor", rstd, rstd, rden, ALU.mult, reads=[ek], writes=[ek])
                        else:
                            OP("act", "activation", rstd, var, AF.Sqrt, bias=EPS, reads=[ek], writes=[ek])
                            OP("dve", "reciprocal", rstd, rstd, reads=[ek], writes=[ek])
                        OP("dve", "scalar_tensor_tensor", nb, mean, -1.0, rstd, ALU.mult, ALU.mult, reads=[ek], writes=[ek])
                        if CUT < 7:
                            continue
                        ynv = yn.rearrange("p (h d) -> p h d", h=6)
                        for hh in range(6):
                            bank = ps[2 + hh // 3]
                            col = (hh % 3) * VW
                            OP("act", "activation", ynv[:, hh, :], bank[:, col:col + 128], AF.Identity, scale=rstd[:, hh:hh + 1], bias=nb[:, hh:hh + 1],
                               reads=["ps%d" % (2 + hh // 3), ek], writes=["yn"])
                        if CUT < 8:
                            continue
                        OP("dve", "tensor_tensor", ybf, yn, gw[:, h0 * 128:(h0 + 6) * 128], ALU.mult, reads=["yn", "gw"], writes=["ybf"])
                        for hh in range(6):
                            OP("pe", "transpose", psb[:, hh * 128:(hh + 1) * 128], ybf[:, hh * 128:(hh + 1) * 128], identb[:], reads=["ybf", "identb"],
                               writes=["psb"])
                        OP("act", "activation", AT[:, h0:h0 + 6, tt * 128:(tt + 1) * 128], psb[:, 0:768].rearrange("p (h d) -> p h d", h=6), AF.Copy,
                           reads=["psb"], writes=["AT"])
                    OP("dve", "tensor_tensor", atot, atot, AA[:, tt, :], ALU.mult, reads=["atot", "tabs"], writes=["atot"])
                if stage == "A":
                    S.dma("sp", outs["sum_S"], Sst, "Sst_o", reads=["Sst"], is_output=True)
                    S.dma("sp", outs["sum_A"], atot, "atot_o", reads=["atot"], is_output=True)
                elif stage == "F" and not full:
                    S.dma("sp", ccsrc[l][:, 0:NH * VW], Sst, "Sst_o", reads=["Sst"], writes=[("ccsrc", l)])
                    S.dma("sp", ccsrc[l][:, NH * VW:NH * VW + NH], atot, "atot_o", reads=["atot"], writes=[("ccsrc", l)])


            def s5(full):
                S.barrier()
                ar.reset()
                sm_off[0] = 0
                KP = 7 + int(round(math.log2(NT)))
                bbT = ar.f32(4096)
                bb_re = bbT[:, 0:2048].rearrange("p (t n) -> p t n", t=16)
                bb_im = bbT[:, 2048:4096].rearrange("p (t n) -> p t n", t=16)
                mark = ar.off
                dk = ["disc"]

                def disc(lre, lim, ldt, n, alloc):
                    dt_ = alloc(n)
                    OP("act", "activation", dt_, ldt, AF.Exp, reads=dk, writes=dk)
                    a = alloc(n)
                    OP("dve", "tensor_tensor", a, lre, dt_, ALU.mult, reads=dk, writes=dk)
                    mag = alloc(n)
                    OP("act", "activation", mag, a, AF.Exp, reads=dk, writes=dk)
                    an = alloc(n)
                    OP("dve", "tensor_tensor", an, lim, dt_, ALU.mult, reads=dk, writes=dk)
                    sinv, cosv, tf_, t2_ = alloc(n), alloc(n), alloc(n), alloc(n)
                    ti_ = alloc(n).bitcast(I32)
                    range_reduce_sin(sinv, an, ti_, tf_, dk, n)
                    cos_from_reduced(cosv, an, t2_, tf_, dk)
                    OP("dve", "tensor_tensor", cosv, cosv, mag, ALU.mult, reads=dk, writes=dk)
                    OP("dve", "tensor_tensor", sinv, sinv, mag, ALU.mult, reads=dk, writes=dk)
                    OP("dve", "tensor_scalar", a, cosv, -1.0, None, ALU.add, reads=dk, writes=dk)
                    OP("dve", "tensor_tensor", dt_, lre, lre, ALU.mult, reads=dk, writes=dk)
                    OP("dve", "tensor_tensor", tf_, lim, lim, ALU.mult, reads=dk, writes=dk)
                    OP("dve", "tensor_tensor", dt_, dt_, tf_, ALU.add, reads=dk, writes=dk)
                    OP("dve", "reciprocal", dt_, dt_, reads=dk, writes=dk)
                    OP("dve", "tensor_tensor", t2_, a, lre, ALU.mult, reads=dk, writes=dk)
                    OP("dve", "tensor_tensor", tf_, sinv, lim, ALU.mult, reads=dk, writes=dk)
                    OP("dve", "tensor_tensor", t2_, t2_, tf_, ALU.add, reads=dk, writes=dk)
                    OP("dve", "tensor_tensor", mag, t2_, dt_, ALU.mult, reads=dk, writes=dk)
                    OP("dve", "tensor_tensor", tf_, sinv, lre, ALU.mult, reads=dk, writes=dk)
                    OP("dve", "tensor_tensor", an, a, lim, ALU.mult, reads=dk, writes=dk)
                    OP("dve", "tensor_tensor", tf_, tf_, an, ALU.subtract, reads=dk, writes=dk)
                    OP("dve", "tensor_tensor", an, tf_, dt_, ALU.mult, reads=dk, writes=dk)
                    return cosv, sinv, mag, an

                for hf in range(2):
                    S.barrier()
                    ar.off = mark
                    c0_ = hf * 1024
                    rep = ar.f32(3 * 1024)
                    S.dma("sp", rep.rearrange("p (a n) -> p a n", a=3), P["s5rep"][:, :, c0_:c0_ + 1024], "s5rep", writes=dk)
                    Bre = ar.f32(1024)
                    Bim = ar.f32(1024)
                    S.dma("sp", Bre, P["s5B"][0].rearrange("p t n -> p (t n)")[:, c0_:c0_ + 1024], "Bre", writes=["Bblk"])
                    S.dma("sp", Bim, P["s5B"][1].rearrange("p t n -> p (t n)")[:, c0_:c0_ + 1024], "Bim", writes=["Bblk"])
                    _, _, wre, wim = disc(rep[:, 0:1024], rep[:, 1024:2048], rep[:, 2048:3072], 1024, ar.f32)
                    t1_ = ar.f32(1024)
                    o_re = bbT[:, c0_:c0_ + 1024]
                    o_im = bbT[:, 2048 + c0_:2048 + c0_ + 1024]
                    OP("dve", "tensor_tensor", o_re, wre, Bre, ALU.mult, reads=dk + ["Bblk"], writes=["bbT"])
                    OP("dve", "tensor_tensor", t1_, wim, Bim, ALU.mult, reads=dk + ["Bblk"], writes=["t1_"])
                    OP("dve", "tensor_tensor", o_re, o_re, t1_, ALU.subtract, reads=["bbT", "t1_"], writes=["bbT"])
                    OP("dve", "tensor_tensor", o_im, wre, Bim, ALU.mult, reads=dk + ["Bblk"], writes=["bbT"])
                    OP("dve", "tensor_tensor", t1_, wim, Bre, ALU.mult, reads=dk + ["Bblk", "t1_"], writes=["t1_"])
                    OP("dve", "tensor_tensor", o_im, o_im, t1_, ALU.add, reads=["bbT", "t1_"], writes=["bbT"])
                S.barrier()
                ar.off = mark
                Em_re = ar.f32(2048).rearrange("p (t n) -> p t n", t=16)
                Em_im = ar.f32(2048).rearrange("p (t n) -> p t n", t=16)
                Ep_re = ar.f32(2048).rearrange("p (t n) -> p t n", t=16)
                nEp_im = ar.f32(2048).rearrange("p (t n) -> p t n", t=16)
                p16 = sm(48)
                S.dma("sp", p16, P["s5p"].rearrange("p a t -> p (a t)"), "s5p", writes=dk)
                lbr, lbi, _, _ = disc(p16[:, 0:16], p16[:, 16:32], p16[:, 32:48], 16, sm)
                pwr = [lbr] + [sm(16) for _ in range(KP)]
                pwi = [lbi] + [sm(16) for _ in range(KP)]
                tq = sm(16)

                def csq(dr, di, sr, si):
                    OP("dve", "tensor_tensor", dr, sr, sr, ALU.mult, reads=dk, writes=dk)
                    OP("dve", "tensor_tensor", tq, si, si, ALU.mult, reads=dk, writes=dk)
                    OP("dve", "tensor_tensor", dr, dr, tq, ALU.subtract, reads=dk, writes=dk)
                    OP("dve", "scalar_tensor_tensor", di, sr, 2.0, si, ALU.mult, ALU.mult, reads=dk, writes=dk)

                for k in range(KP):
                    csq(pwr[k + 1], pwi[k + 1], pwr[k], pwi[k])
                mur = [sm(16) for _ in range(7)]
                mui = [sm(16) for _ in range(7)]
                OP("dve", "tensor_tensor", tq, lbr, lbr, ALU.mult, reads=dk, writes=dk)
                OP("dve", "tensor_tensor", mur[0], lbi, lbi, ALU.mult, reads=dk, writes=dk)
                OP("dve", "tensor_tensor", tq, tq, mur[0], ALU.add, reads=dk, writes=dk)
                OP("dve", "reciprocal", tq, tq, reads=dk, writes=dk)
                OP("dve", "tensor_tensor", mur[0], lbr, tq, ALU.mult, reads=dk, writes=dk)
                OP("dve", "scalar_tensor_tensor", mui[0], lbi, -1.0, tq, ALU.mult, ALU.mult, reads=dk, writes=dk)
                for k in range(6):
                    csq(mur[k + 1], mui[k + 1], mur[k], mui[k])
                mark2 = ar.off
                et1 = ar.f32(16 * 64).rearrange("p (t n) -> p t n", t=16)
                et2 = ar.f32(16 * 64).rearrange("p (t n) -> p t n", t=16)

                def build_table(Er, Ei, pr, pi):
                    OP("dve", "memset", Er[:, :, 0:1], 1.0, reads=dk, writes=dk)
                    OP("dve", "memset", Ei[:, :, 0:1], 0.0, reads=dk, writes=dk)
                    for k in range(7):
                        n = 1 << k
                        brd = lambda v: v.unsqueeze(2).to_broadcast([128, 16, n])
                        a1, a2 = et1[:, :, 0:n], et2[:, :, 0:n]
                        OP("dve", "tensor_tensor", a1, Er[:, :, 0:n], brd(pr[k]), ALU.mult, reads=dk, writes=dk)
                        OP("dve", "tensor_tensor", a2, Ei[:, :, 0:n], brd(pi[k]), ALU.mult, reads=dk, writes=dk)
                        OP("dve", "tensor_tensor", Er[:, :, n:2 * n], a1, a2, ALU.subtract, reads=dk, writes=dk)
                        OP("dve", "tensor_tensor", a1, Er[:, :, 0:n], brd(pi[k]), ALU.mult, reads=dk, writes=dk)
                        OP("dve", "tensor_tensor", a2, Ei[:, :, 0:n], brd(pr[k]), ALU.mult, reads=dk, writes=dk)
                        OP("dve", "tensor_tensor", Ei[:, :, n:2 * n], a1, a2, ALU.add, reads=dk, writes=dk)

                build_table(Ep_re, nEp_im, pwr, pwi)
                OP("dve", "tensor_scalar", nEp_im, nEp_im, -1.0, None, ALU.mult, reads=dk, writes=dk)
                build_table(Em_re, Em_im, mur, mui)
                L_re, L_im = pwr[7], pwi[7]
                S.barrier()
                ar.off = mark2
                CAR = sm(32)
                CARv = CAR.rearrange("p (k t) -> p k t", k=2)
                OP("dve", "memset", CAR, 0.0, writes=["CAR"])
                if full:
                    gx = [sm(32), sm(32)]
                    le = sm(32)
                    cn = sm(32)
                    ct1 = sm(16)
                    for j in range(NCORES):
                        sl = j % 2
                        S.dma("sp", gx[sl], gsrc(l, "g_X", j), "gx%d" % sl, reads=[("ccdst", l)], writes=["gx%d" % sl])
                        ck = ["CAR", "cix"]
                        mj = CMASK[:, j:j + 1]
                        OP("dve", "tensor_scalar", le[:, 0:16], pwr[KP], -1.0, mj, ALU.add, ALU.mult, reads=dk + ck + ["CMASK"], writes=ck)
                        OP("dve", "tensor_scalar", le[:, 0:16], le[:, 0:16], 1.0, None, ALU.add, reads=ck, writes=ck)
                        OP("dve", "tensor_scalar", le[:, 16:32], pwi[KP], mj, None, ALU.mult, reads=dk + ck + ["CMASK"], writes=ck)
                        OP("dve", "tensor_scalar", gx[sl], gx[sl], mj, None, ALU.mult, reads=["gx%d" % sl, "CMASK"], writes=["gx%d" % sl])
                        OP("dve", "tensor_tensor", cn[:, 0:16], le[:, 0:16], CAR[:, 0:16], ALU.mult, reads=ck, writes=ck)
                        OP("dve", "tensor_tensor", ct1, le[:, 16:32], CAR[:, 16:32], ALU.mult, reads=ck, writes=ck)
                        OP("dve", "tensor_tensor", cn[:, 0:16], cn[:, 0:16], ct1, ALU.subtract, reads=ck, writes=ck)
                        OP("dve", "tensor_tensor", cn[:, 16:32], le[:, 0:16], CAR[:, 16:32], ALU.mult, reads=ck, writes=ck)
                        OP("dve", "tensor_tensor", ct1, le[:, 16:32], CAR[:, 0:16], ALU.mult, reads=ck, writes=ck)
                        OP("dve", "tensor_tensor", cn[:, 16:32], cn[:, 16:32], ct1, ALU.add, reads=ck, writes=ck)
                        OP("dve", "tensor_tensor", CAR, cn, gx[sl], ALU.add, reads=ck + ["gx%d" % sl], writes=ck)
                usl = [ar.f32(4 * TB), ar.f32(4 * TB)]
                rmask = ar.f32(TB)
                OP("pool", "memset", rmask, 1.0, writes=["rmask"])
                OP("pool", "memset", rmask.rearrange("p (c n) -> p c n", n=128)[:, :, 0:1], 0.0, reads=["rmask"], writes=["rmask"])
                zt1 = ar.f32(TB)
                zt2 = ar.f32(TB)
                zz = ar.f32(2 * TB)
                ww = ar.f32(2 * TB)
                xx = ar.f32(2 * TB)
                wwv = ww.rearrange("p (k n) -> p k n", k=2)
                cinb = [sm(2 * (TPB + 1)), sm(2 * (TPB + 1))]
                cp = [sm(4), sm(4)]
                if full:
                    Cblk = [ar.f32(2 * 512), ar.f32(2 * 512)]
                    ysg = ar.f32(4 * TB)
                    ysb = ar.bf16(4 * TB)
                    sgb = ar.f32(TB)
                    gluw = ar.bf16(4 * 512)
                    gluwv = gluw.rearrange("p (k n) -> p k n", k=4)
                    S.dma("pool", gluwv, P["glu_w"].rearrange("(k p) n -> p k n", p=128), "gluw", writes=["gluw"])
                    s5d = sm(4)
                    glub = sm(4)
                    S.dma("sp", s5d, P["s5D"], "s5d", writes=["s5d"])
                    S.dma("sp", glub, P["glu_b"], "glub", writes=["glub"])
                bc4 = lambda v: v.unsqueeze(1).to_broadcast([128, TPB, 128])
                v4 = lambda v: v.rearrange("p (c n) -> p c n", n=128)
                for tb in range(NTB):
                    us = usl[tb % 2]
                    uk = "usl%d" % (tb % 2)
                    usv = us.rearrange("p (c n) -> p c n", c=4)
                    S.dma("sp", usv, UT[:, :, tb * TB:(tb + 1) * TB].rearrange("c p n -> p c n"), uk, reads=["UT"], writes=[uk])
                    for t in range(16):
                        ct = t // 4
                        if full and t % 4 == 0:
                            cb_ = Cblk[ct % 2]
                            cbk = "Cblk%d" % (ct % 2)
                            for k in range(2):
                                S.dma("sp", cb_[:, k * 512:(k + 1) * 512].rearrange("p (t c) -> p t c", t=4), P["s5C"][k][:, 4 * ct:4 * ct + 4, :], cbk,
                                      writes=[cbk])
                        pr_, pi_ = ps[2 * (t % 2)], ps[2 * (t % 2) + 1]
                        prk, pik = "ps%d" % (2 * (t % 2)), "ps%d" % (2 * (t % 2) + 1)
                        OP("pe", "matmul", pr_[:, 0:TB], bb_re[:, t, :], usv[:, ct, :], start=True, stop=True, reads=["bbT", uk], writes=[prk])
                        OP("pe", "matmul", pi_[:, 0:TB], bb_im[:, t, :], usv[:, ct, :], start=True, stop=True, reads=["bbT", uk], writes=[pik])
                        zk = ["zz", "zt"]
                        OP("dve", "tensor_tensor", v4(zt1), v4(pr_[:, 0:TB]), bc4(Em_re[:, t, :]), ALU.mult, reads=[prk] + dk + zk, writes=["zt"])
                        OP("dve", "tensor_tensor", v4(zt2), v4(pi_[:, 0:TB]), bc4(Em_im[:, t, :]), ALU.mult, reads=[pik] + dk + zk, writes=["zt"])
                        OP("pool", "tensor_tensor", zz[:, 0:TB], zt1, zt2, ALU.subtract, reads=["zt", "zz"], writes=["zz"])
                        OP("dve", "tensor_tensor", v4(zt1), v4(pr_[:, 0:TB]), bc4(Em_im[:, t, :]), ALU.mult, reads=[prk] + dk + zk, writes=["zt"])
                        OP("dve", "tensor_tensor", v4(zt2), v4(pi_[:, 0:TB]), bc4(Em_re[:, t, :]), ALU.mult, reads=[pik] + dk + zk, writes=["zt"])
                        OP("pool", "tensor_tensor", zz[:, TB:2 * TB], zt1, zt2, ALU.add, reads=["zt", "zz"], writes=["zz"])
                        OP("dve", "tensor_tensor_scan", ww[:, 0:TB], rmask, zz[:, 0:TB], 0.0, ALU.mult, ALU.add, reads=["zz", "rmask", "ww"], writes=["ww"])
                        OP("dve", "tensor_tensor_scan", ww[:, TB:2 * TB], rmask, zz[:, TB:2 * TB], 0.0, ALU.mult, ALU.add, reads=["zz", "rmask", "ww"],
                           writes=["ww"])
                        cbv = cinb[t % 2].rearrange("p (k c) -> p k c", k=2)
                        cpv = cp[t % 2]
                        kc_ = ["cinb%d" % (t % 2)]
                        OP("pool", "tensor_copy", cbv[:, :, 0], CARv[:, :, t], reads=["CAR"] + kc_, writes=kc_)
                        for cc in range(TPB):
                            col = cc * 128 + 127
                            OP("pool", "tensor_tensor", cpv[:, 0:2], wwv[:, :, col], cbv[:, :, cc], ALU.add, reads=["ww"] + kc_, writes=kc_)
                            OP("pool", "tensor_scalar", cpv[:, 2:4], cpv[:, 0:2], L_im[:, t:t + 1], None, ALU.mult, reads=kc_ + dk, writes=kc_)
                            OP("pool", "tensor_scalar", cpv[:, 0:2], cpv[:, 0:2], L_re[:, t:t + 1], None, ALU.mult, reads=kc_ + dk, writes=kc_)
                            OP("pool", "tensor_tensor", cbv[:, 0, cc + 1:cc + 2], cpv[:, 0:1], cpv[:, 3:4], ALU.subtract, reads=kc_, writes=kc_)
                            OP("pool", "tensor_tensor", cbv[:, 1, cc + 1:cc + 2], cpv[:, 2:3], cpv[:, 1:2], ALU.add, reads=kc_, writes=kc_)
                        OP("pool", "tensor_copy", CARv[:, :, t], cbv[:, :, TPB], reads=kc_ + ["CAR"], writes=["CAR"])
                        if not full:
                            continue
                        OP("dve", "tensor_tensor", ww.rearrange("p (k c n) -> p k c n", k=2, n=128), ww.rearrange("p (k c n) -> p k c n", k=2, n=128),
                           cbv[:, :, 0:TPB].unsqueeze(3).to_broadcast([128, 2, TPB, 128]), ALU.add, reads=["ww"] + kc_, writes=["ww"])
                        OP("dve", "tensor_tensor", v4(zt1), v4(ww[:, 0:TB]), bc4(Ep_re[:, t, :]), ALU.mult, reads=["ww", "zt", "zz"] + dk, writes=["zt"])
                        OP("dve", "tensor_tensor", v4(zt2), v4(ww[:, TB:2 * TB]), bc4(nEp_im[:, t, :]), ALU.mult, reads=["ww", "zt", "zz"] + dk, writes=["zt"])
                        OP("pool", "tensor_tensor", xx[:, 0:TB], zt1, zt2, ALU.add, reads=["zt", "xx"], writes=["xx"])
                        OP("dve", "tensor_tensor", v4(zt1), v4(ww[:, 0:TB]), bc4(nEp_im[:, t, :]), ALU.mult, reads=["ww", "zt", "xx"] + dk, writes=["zt"])
                        OP("dve", "tensor_tensor", v4(zt2), v4(ww[:, TB:2 * TB]), bc4(Ep_re[:, t, :]), ALU.mult, reads=["ww", "zt", "xx"] + dk, writes=["zt"])
                        OP("pool", "tensor_tensor", xx[:, TB:2 * TB], zt1, zt2, ALU.subtract, reads=["zt", "xx"], writes=["xx"])
                        py = ps[4 + ct % 2]
                        pyk = "ps%d" % (4 + ct % 2)
                        cb_ = Cblk[ct % 2]
                        cbk = "Cblk%d" % (ct % 2)
                        tl = t % 4
                        OP("pe", "matmul", py[:, 0:TB], cb_[:, tl * 128:(tl + 1) * 128], xx[:, 0:TB], start=(tl == 0), stop=False,
                           reads=[cbk, "xx"], writes=[pyk])
                        OP("pe", "matmul", py[:, 0:TB], cb_[:, 512 + tl * 128:512 + (tl + 1) * 128], xx[:, TB:2 * TB], start=False, stop=(tl == 3),
                           reads=[cbk, "xx"], writes=[pyk])
                        if tl == 3:
                            OP("dve", "scalar_tensor_tensor", ysg[:, ct * TB:(ct + 1) * TB], usv[:, ct, :], s5d[:, ct:ct + 1], py[:, 0:TB], ALU.mult, ALU.add,
                               reads=[uk, "s5d", pyk, "ysg"], writes=["ysg"])
                            OP("act", "activation", ysg[:, ct * TB:(ct + 1) * TB], ysg[:, ct * TB:(ct + 1) * TB], AF.Gelu, reads=["ysg"], writes=["ysg"])
                            OP("act", "activation", ysb[:, ct * TB:(ct + 1) * TB], ysg[:, ct * TB:(ct + 1) * TB], AF.Copy, reads=["ysg", "ysb"], writes=["ysb"])
                    if full:
                        for oc in range(4):
                            for kc in range(4):
                                OP("pe", "matmul", ps[6][:, 0:TB], gluwv[:, kc, oc * 128:(oc + 1) * 128], ysb[:, kc * TB:(kc + 1) * TB], start=(kc == 0), stop=(kc == 3),
                                   reads=["gluw", "ysb"], writes=["ps6"])
                            OP("act", "activation", sgb, ps[6][:, 0:TB], AF.Sigmoid, bias=glub[:, oc:oc + 1], reads=["ps6", "glub", "sgb"], writes=["sgb"])
                            OP("dve", "tensor_tensor", AT[:, 12 + oc, tb * TB:(tb + 1) * TB], ysg[:, oc * TB:(oc + 1) * TB], sgb, ALU.mult,
                               reads=["ysg", "sgb"], writes=["AT"])
                if stage == "A":
                    S.dma("sp", outs["sum_X"], CAR, "CAR_o", reads=["CAR"], is_output=True)
                elif stage == "F" and not full:
                    S.dma("sp", ccsrc[l][:, NH * VW + NH:SUMW], CAR, "CAR_o", reads=["CAR"], writes=[("ccsrc", l)])

            def rest():
                def ln_phase(src, iw, dst_tiles, to_AT, is_out):
                    S.barrier()
                    ar.reset()
                    sm_off[0] = 0
                    lnw = ar.f32(D)
                    lnb = ar.f32(D)
                    S.dma("sp", lnw, P["lnp"][iw], "lnw", writes=["lnw"])
                    S.dma("sp", lnb, P["lnp"][iw + 1], "lnb", writes=["lnb"])
                    rs = [ar.f32(D), ar.f32(D)]
                    xo = [ar.f32(D), ar.f32(D)]
                    stt = [sm(24), sm(24)]
                    mvv = [sm(4), sm(4)]
                    for tt in range(NT):
                        sl = tt % 2
                        rk_, xk, lk = "rs%d" % sl, "xo%d" % sl, "lst%d" % sl
                        S.dma("sp", rs[sl], src[tt], rk_, reads=[(src.name if hasattr(src, "name") else "src", tt)], writes=[rk_])
                        for c in range(4):
                            OP("dve", "bn_stats", stt[sl][:, c * 6:(c + 1) * 6], rs[sl][:, c * 512:(c + 1) * 512], reads=[rk_, lk], writes=[lk])
                        OP("dve", "bn_aggr", mvv[sl][:, 0:2], stt[sl], reads=[lk], writes=[lk])
                        OP("act", "activation", mvv[sl][:, 2:3], mvv[sl][:, 1:2], AF.Sqrt, bias=EPS, reads=[lk], writes=[lk])
                        OP("dve", "reciprocal", mvv[sl][:, 2:3], mvv[sl][:, 2:3], reads=[lk], writes=[lk])
                        OP("dve", "scalar_tensor_tensor", mvv[sl][:, 3:4], mvv[sl][:, 0:1], -1.0, mvv[sl][:, 2:3], ALU.mult, ALU.mult, reads=[lk], writes=[lk])
                        OP("act", "activation", xo[sl], rs[sl], AF.Identity, scale=mvv[sl][:, 2:3], bias=mvv[sl][:, 3:4], reads=[rk_, lk, xk], writes=[xk])
                        OP("dve", "tensor_tensor", xo[sl], xo[sl], lnw, ALU.mult, reads=[xk, "lnw"], writes=[xk])
                        OP("pool", "tensor_tensor", xo[sl], xo[sl], lnb, ALU.add, reads=[xk, "lnb"], writes=[xk])
                        S.dma("sp", dst_tiles(tt), xo[sl], xk, reads=[xk], writes=[("lnout", iw, tt)], is_output=is_out)
                        if to_AT:
                            for g in range(4):
                                b = ps[g]
                                for j in range(4):
                                    kc = 4 * g + j
                                    OP("pe", "transpose", b[:, j * 128:(j + 1) * 128], xo[sl][:, kc * 128:(kc + 1) * 128], identf,
                                       reads=[xk, "CST"], writes=["ps%d" % g])
                                dstv = AT[:, 4 * g:4 * g + 4, tt * 128:(tt + 1) * 128]
                                srcv = b[:].rearrange("p (a b) -> p a b", a=4)
                                if g % 2 == 0:
                                    OP("act", "activation", dstv, srcv, AF.Copy, reads=["ps%d" % g], writes=["AT"])
                                else:
                                    OP("dve", "tensor_copy", dstv, srcv, reads=["ps%d" % g], writes=["AT"])

                S.barrier()
                ar.reset()
                wsl = [ar.bf16(16 * 512), ar.bf16(16 * 512)]
                xres = [ar.f32(512), ar.f32(512)]
                rstg = [ar.f32(512), ar.f32(512)]
                it = 0
                for g in range(4):
                    wv, wkey = load_w(wsl, g, P["w_out"], g * 512, 512, 16)
                    for tt in range(NT):
                        b, sl = it % 4, it % 2
                        it += 1
                        pa, pk = ps[b], "ps%d" % b
                        S.dma("sp", xres[sl], x_tiles(tt)[:, g * 512:(g + 1) * 512], "xres%d" % sl, writes=["xres%d" % sl])
                        for kc in range(16):
                            OP("pe", "matmul", pa[:, 0:512], AT[:, kc, tt * 128:(tt + 1) * 128], wv[:, kc, :], start=(kc == 0), stop=(kc == 15),
                               reads=["AT", wkey], writes=[pk])
                        OP("dve", "scalar_tensor_tensor", rstg[sl], xres[sl], ALPHA, pa[:, 0:512], ALU.mult, ALU.add, reads=["xres%d" % sl, pk, "rstg%d" % sl],
                           writes=["rstg%d" % sl])
                        S.dma("sp", R1[tt, :, g * 512:(g + 1) * 512], rstg[sl], "rstg%d" % sl, reads=["rstg%d" % sl], writes=[("R1", tt)])
                ln_phase(R1, 0, lambda tt: X1[tt], True, False)

                S.barrier()
                ar.reset()
                wsl = [ar.bf16(16 * 512), ar.bf16(16 * 512)]
                hrelu = [ar.f32(TB), ar.f32(TB)]
                hstg = [ar.bf16(4 * T), ar.bf16(4 * T)]
                it = 0
                for g in range(16):
                    wv, wkey = load_w(wsl, g, P["w_up"], g * 512, 512, 16)
                    hs = hstg[g % 2]
                    hk = "hstg%d" % (g % 2)
                    for fi in range(4):
                        for tb in range(NTB):
                            b, sl = it % 4, it % 2
                            it += 1
                            pa, pk = ps[b], "ps%d" % b
                            for kc in range(16):
                                OP("pe", "matmul", pa[:, 0:TB], wv[:, kc, fi * 128:(fi + 1) * 128], AT[:, kc, tb * TB:(tb + 1) * TB],
                                   start=(kc == 0), stop=(kc == 15), reads=["AT", wkey], writes=[pk])
                            OP("act", "activation", hrelu[sl], pa[:, 0:TB], AF.Relu, reads=[pk, "hrelu%d" % sl], writes=["hrelu%d" % sl])
                            OP("dve", "tensor_tensor", hs[:, fi * T + tb * TB:fi * T + (tb + 1) * TB], hrelu[sl], hrelu[sl], ALU.mult,
                               reads=["hrelu%d" % sl, hk], writes=[hk])
                    for k4 in range(4):
                        S.dma("sp", HH[:, :, 4 * g + k4, :].rearrange("t p d -> p t d"), hs[:, k4 * T:(k4 + 1) * T].rearrange("p (t d) -> p t d", d=128), hk,
                              reads=[hk], writes=["HH"])
                S.barrier()
                ar.reset()
                wdl = [ar.bf16(64 * 256), ar.bf16(64 * 256)]
                hsl = [ar.bf16(64 * 128), ar.bf16(64 * 128)]
                x1s = [ar.f32(256), ar.f32(256)]
                r2s = [ar.f32(256), ar.f32(256)]
                it = 0
                for g in range(8):
                    wv, wkey = load_w(wdl, g, P["w_down"], g * 256, 256, 64, key="wd")
                    for tt in range(NT):
                        b, sl = it % 4, it % 2
                        it += 1
                        pa, pk = ps[b], "ps%d" % b
                        S.dma("sp", hsl[sl], HH[tt].rearrange("p k d -> p (k d)"), "hsl%d" % sl, reads=["HH"], writes=["hsl%d" % sl])
                        S.dma("sp", x1s[sl], X1[tt, :, g * 256:(g + 1) * 256], "x1s%d" % sl, reads=[("lnout", 0, tt)], writes=["x1s%d" % sl])
                        hv = hsl[sl].rearrange("p (k d) -> p k d", k=64)
                        for kc in range(64):
                            OP("pe", "matmul", pa[:, 0:256], hv[:, kc, :], wv[:, kc, :], start=(kc == 0), stop=(kc == 63),
                               reads=["hsl%d" % sl, wkey], writes=[pk])
                        OP("dve", "scalar_tensor_tensor", r2s[sl], x1s[sl], ALPHA, pa[:, 0:256], ALU.mult, ALU.add, reads=["x1s%d" % sl, pk, "r2s%d" % sl],
                           writes=["r2s%d" % sl])
                        S.dma("sp", R2[tt, :, g * 256:(g + 1) * 256], r2s[sl], "r2s%d" % sl, reads=["r2s%d" % sl], writes=[("R2", tt)])

                load_AT_from(lambda tt: R2[tt])
                S.barrier()
                ar.reset()
                pst = [ar.f32(PLE), ar.f32(PLE)]
                for tt in range(NT):
                    sl = tt % 2
                    S.dma("sp", pst[sl], P["p"][tt * 128:(tt + 1) * 128, :], "pst%d" % sl, writes=["pst%d" % sl])
                    for kc in range(2):
                        OP("pe", "transpose", ps[4][:, kc * 128:(kc + 1) * 128], pst[sl][:, kc * 128:(kc + 1) * 128], identf, reads=["pst%d" % sl, "CST"],
                           writes=["ps4"])
                    OP("dve", "tensor_copy", ATP[:, :, tt * 128:(tt + 1) * 128], ps[4][:, 0:256].rearrange("p (a b) -> p a b", a=2), reads=["ps4"], writes=["ATP"])
                wsl = [ar.bf16(16 * 512), ar.bf16(16 * 512)]
                wpl = ar.bf16(2 * D)
                wplv = wpl.rearrange("p (k n) -> p k n", k=2)
                S.dma("pool", wplv, P["w_ple"].rearrange("(k p) n -> p k n", p=128), "wpl", writes=["wpl"])
                sgs = [ar.f32(512), ar.f32(512)]
                r2l = [ar.f32(512), ar.f32(512)]
                r3s = [ar.f32(512), ar.f32(512)]
                it = 0
                for g in range(4):
                    wv, wkey = load_w(wsl, g, P["w_gate"], g * 512, 512, 16)
                    for tt in range(NT):
                        sl = it % 2
                        pa, pk = ps[2 * (it % 2)], "ps%d" % (2 * (it % 2))
                        pb, pbk = ps[2 * (it % 2) + 1], "ps%d" % (2 * (it % 2) + 1)
                        it += 1
                        S.dma("sp", r2l[sl], R2[tt, :, g * 512:(g + 1) * 512], "r2l%d" % sl, reads=[("R2", tt)], writes=["r2l%d" % sl])
                        for kc in range(16):
                            OP("pe", "matmul", pa[:, 0:512], AT[:, kc, tt * 128:(tt + 1) * 128], wv[:, kc, :], start=(kc == 0), stop=(kc == 15),
                               reads=["AT", wkey], writes=[pk])
                        for kc in range(2):
                            OP("pe", "matmul", pb[:, 0:512], ATP[:, kc, tt * 128:(tt + 1) * 128], wplv[:, kc, g * 512:(g + 1) * 512], start=(kc == 0), stop=(kc == 1),
                               reads=["ATP", "wpl"], writes=[pbk])
                        OP("act", "activation", sgs[sl], pa[:, 0:512], AF.Sigmoid, reads=[pk, "sgs%d" % sl], writes=["sgs%d" % sl])
                        OP("dve", "tensor_tensor", sgs[sl], pb[:, 0:512], sgs[sl], ALU.mult, reads=[pbk, "sgs%d" % sl], writes=["sgs%d" % sl])
                        OP("pool", "tensor_tensor", r3s[sl], sgs[sl], r2l[sl], ALU.add, reads=["sgs%d" % sl, "r2l%d" % sl, "r3s%d" % sl], writes=["r3s%d" % sl])
                        S.dma("sp", R3[tt, :, g * 512:(g + 1) * 512], r3s[sl], "r3s%d" % sl, reads=["r3s%d" % sl], writes=[("R3", tt)])
                ln_phase(R3, 2, out_tiles, False, out_is_ext)

            if stage == "A":
                pm(False)
                s5(False)
            elif stage == "B":
                pm(True)
                s5(True)
                rest()
            else:
                pm(False)
                s5(False)
                OP("pool", "collective_compute", "AllGather", ALU.bypass, replica_groups=[list(range(NCORES))],
                   ins=[ccsrc[l].opt()], outs=[ccdst[l].opt()], reads=[("ccsrc", l)], writes=[("ccdst", l)])
                pm(True)
                s5(True)
                rest()

        if NL == 1:
            layer(0, lambda tt: x_in[tt * 128:(tt + 1) * 128, :], xh_in, (lambda tt: outs["y"][tt * 128:(tt + 1) * 128, :]) if stage != "A" else None, True)
        else:
            layer(0, lambda tt: x_in[tt * 128:(tt + 1) * 128, :], xh_in, lambda tt: XB[tt], False)
            S.barrier()
            ar.reset()
            hb = ar.f32(D)
            hg = ar.f32(NCORES * D)
            hacc = ar.f32(D)
            OHM = ar.f32(NCORES)
            S.dma("sp", OHM, oh_in, "OHM", writes=["OHM"])
            S.dma("sp", hb[0:4, :], XB[NT - 1, 124:128, :], "hb", writes=["hb"])
            S.dma("sp", hsrc, hb[0:4, :], "hb", reads=["hb"], writes=["hsrc"])
            OP("pool", "collective_compute", "AllGather", ALU.bypass, replica_groups=[list(range(NCORES))],
               ins=[hsrc.opt()], outs=[hdst.opt()], reads=["hsrc"], writes=["hdst"])
            S.dma("sp", hg[0:4, :].rearrange("p (r n) -> p r n", r=NCORES), hdst.rearrange("(r p) n -> p r n", p=4), "hg", reads=["hdst"], writes=["hg"])
            OP("dve", "memset", hacc[0:4, :], 0.0, writes=["hacc"])
            for j in range(NCORES):
                OP("dve", "scalar_tensor_tensor", hacc[0:4, :], hg[0:4, j * D:(j + 1) * D], OHM[0:4, j:j + 1], hacc[0:4, :], ALU.mult, ALU.add,
                   reads=["hg", "OHM", "hacc"], writes=["hacc"])
            S.dma("sp", halo1, hacc[0:4, :], "hacc", reads=["hacc"], writes=["halo1"])
            layer(1, lambda tt: XB[tt], halo1, lambda tt: outs["y"][tt * 128:(tt + 1) * 128, :], True)
        S.finish()
        S.emit()
    return nc, outs, dbg_out


def host_consts():
    c = np.zeros((128, 128 * 4 + 64 + 18), np.float32)
    c[:, 0:128] = np.eye(128, dtype=np.float32)
    j = np.arange(128)[:, None]
    i = np.arange(128)[None, :]
    c[:, 128:256] = (j <= i).astype(np.float32)
    c[:, 256:384] = 1.0
    invf = (np.float32(10000.0) ** (-(np.arange(0, 128, 2, dtype=np.float32) / np.float32(128.0)))).astype(np.float32)
    c[:, 512:576] = invf[None, :]
    lg = np.log(1.0 - 2.0 ** (-5.0 - np.arange(6, dtype=np.float64)))
    idx = np.arange(128, dtype=np.float64)[:, None]
    c[:, 576:582] = np.exp(lg[None, :] * (idx + 1.0))
    c[:, 582:588] = np.exp(-lg[None, :] * (idx + 1.0)) * 128.0 ** -0.5
    c[:, 588:594] = np.exp(lg[None, :] * 128.0)
    return c


def _s5_state_layout(a):
    return np.ascontiguousarray(a.reshape(16, 2, 64).transpose(1, 2, 0).reshape(128, 16))


def host_layer_arrays(inp, l):
    f = np.float32
    rep = lambda v: np.ascontiguousarray(np.broadcast_to(np.asarray(v, f)[None, :], (128, np.asarray(v).shape[-1])))
    d = {}
    d["w_in"] = np.asarray(inp["w_in"][l], f)
    cw = np.asarray(inp["mlstm_conv_w"][l], f)
    d["cw"] = np.ascontiguousarray(cw.reshape(4, 12, 128).transpose(2, 1, 0))
    d["cb"] = np.ascontiguousarray(np.asarray(inp["mlstm_conv_b"][l], f).reshape(12, 128).T)
    d["gbias"] = rep(np.concatenate([np.asarray(inp["mlstm_i_bias"][l], f), np.asarray(inp["mlstm_f_bias"][l], f)]))
    d["hnw"] = rep(np.concatenate([np.asarray(inp["ret_norm_w"][l], f), np.asarray(inp["mlstm_norm_w"][l], f)]))
    lre = np.asarray(inp["s5_lambda_re"][l], f)
    lim = np.asarray(inp["s5_lambda_im"][l], f)
    ldt = np.broadcast_to(np.asarray(inp["s5_log_dt"][l], f)[:, None], (32, 64))
    s5p = np.stack([_s5_state_layout(lre), _s5_state_layout(lim), _s5_state_layout(ldt)], axis=1)
    d["s5p"] = np.ascontiguousarray(s5p)
    flat = lambda a: np.ascontiguousarray(_s5_state_layout(a).T.reshape(2048))
    s5rep = np.stack([flat(lre), flat(lim), flat(ldt)], axis=0)
    d["s5rep"] = np.ascontiguousarray(np.broadcast_to(s5rep[None], (128, 3, 2048)))
    Bb = np.zeros((2, 128, 16, 128), f)
    Cb = np.zeros((2, 128, 16, 128), f)
    Bs = [np.asarray(inp["s5_B_re"][l], f), np.asarray(inp["s5_B_im"][l], f)]
    Cs = [np.asarray(inp["s5_C_re"][l], f), np.asarray(inp["s5_C_im"][l], f)]
    for t in range(16):
        for s in range(2):
            g = 2 * t + s
            r0 = (g % 8) * 16
            for k in range(2):
                Bb[k, r0:r0 + 16, t, s * 64:(s + 1) * 64] = Bs[k][g].T
                Cb[k, s * 64:(s + 1) * 64, t, r0:r0 + 16] = Cs[k][g].T
    d["s5B"] = Bb
    d["s5C"] = Cb
    d["s5D"] = np.ascontiguousarray(np.asarray(inp["s5_D"][l], f).reshape(4, 128).T)
    d["glu_w"] = np.asarray(inp["s5_glu_w"][l], f)
    d["glu_b"] = np.ascontiguousarray(np.asarray(inp["s5_glu_b"][l], f).reshape(4, 128).T)
    d["w_out"] = np.asarray(inp["w_out"][l], f)
    d["lnp"] = np.ascontiguousarray(np.stack([rep(inp["ln1_w"][l]), rep(inp["ln1_b"][l]), rep(inp["ln2_w"][l]), rep(inp["ln2_b"][l])], 0))
    d["w_up"] = np.asarray(inp["w_up"][l], f)
    d["w_down"] = np.asarray(inp["w_down"][l], f)
    d["w_gate"] = np.asarray(inp["w_gate"][l], f)
    d["w_ple"] = np.asarray(inp["w_ple"][l], f)
    return d


_PROGS = {}


def _prog(stage):
    if stage not in _PROGS:
        _PROGS[stage] = build(SEQ // NCORES // 128, 1, stage)[0]
    return _PROGS[stage]


A_KEYS = ("w_in", "cw", "cb", "gbias", "s5p", "s5rep", "s5B")


def kernel(**inputs):
    inp = {k: np.asarray(v) for k, v in inputs.items()}
    x = np.ascontiguousarray(inp["x"][0], dtype=np.float32)
    pos = np.asarray(inp["positions"][0], dtype=np.int32)
    TC = SEQ // NCORES
    NT = TC // 128
    consts = host_consts()
    if "F" not in _PROGS:
        _PROGS["F"] = build(NT, 2, "F")[0]
    nc = _PROGS["F"]
    las = [host_layer_arrays(inp, l) for l in range(2)]
    maps = []
    for c in range(NCORES):
        cm = np.zeros((128, NCORES), np.float32)
        cm[:, :c] = 1.0
        oh = np.zeros((128, NCORES), np.float32)
        if c > 0:
            oh[:, c - 1] = 1.0
        m = {"x": np.ascontiguousarray(x[c * TC:(c + 1) * TC]),
             "xh": np.ascontiguousarray(x[c * TC - 4:c * TC]) if c > 0 else np.zeros((4, D), np.float32),
             "pos": np.ascontiguousarray(pos[c * TC:(c + 1) * TC].reshape(NT, 128).T), "cmask": cm, "oh": oh, "consts": consts}
        for l in range(2):
            m["p%d" % l] = np.ascontiguousarray(np.asarray(inp["p"][l, 0, c * TC:(c + 1) * TC], np.float32))
            for k, v in las[l].items():
                m[k + str(l)] = v
        maps.append(m)
    res = run_bass_kernel_spmd(nc, maps, core_ids=list(range(NCORES)))
    return np.concatenate([np.asarray(r["y"], np.float32) for r in res.results], 0)[None].astype(np.float32)
```

```python
import math
from contextlib import ExitStack
import numpy as np
import concourse.bass as bass
import concourse.mybir as mybir
from concourse.bass_utils import run_bass_kernel_spmd

F32 = mybir.dt.float32
BF16 = mybir.dt.bfloat16
I32 = mybir.dt.int32
AF = mybir.ActivationFunctionType
ALU = mybir.AluOpType

NCORES = 8
SEQ = 16384
D = 2048
DFF = 8192
PLE = 256
INW = 6668
HD = 128
NH = 12
VW = 130
ALPHA = (2 * 2) ** 0.25
EPS = 1e-5
C_RQ, C_RK, C_RV, C_RG, C_MQ, C_MK, C_MV, C_MO, C_MI, C_MF, C_SU = 0, 768, 1536, 2304, 3072, 3840, 4608, 5376, 6144, 6150, 6156
TWO_PI = 2.0 * math.pi
CW1 = 6.28125
CW2 = TWO_PI - 6.28125


class Sched:
    ENGS = ("pe", "act", "dve", "pool", "sp")

    def __init__(self, nc, same_engine_sync=True):
        self.nc = nc
        self.lists = {e: [] for e in self.ENGS}
        self.cnt = {}
        self.seen = {e: {} for e in self.ENGS}
        self.last_write = {}
        self.readers = {}
        self.sem_names = []
        self.same_engine_sync = same_engine_sync
        self.out_tokens = []
        for e in self.ENGS:
            self._sem("E_" + e)

    def _sem(self, name):
        if name not in self.cnt:
            self.cnt[name] = 0
            self.sem_names.append(name)
        return name

    def _deps(self, e, reads, writes, own_dma_sem=None):
        toks = {}

        def add(t):
            if t is None:
                return
            s, v = t
            if toks.get(s, 0) < v:
                toks[s] = v
        for b in reads:
            add(self.last_write.get(b))
        for b in writes:
            lw = self.last_write.get(b)
            if lw is not None and not (own_dma_sem is not None and lw[0] == own_dma_sem):
                add(lw)
            for t in self.readers.get(b, ()):
                add(t)
        own = "E_" + e
        for s, v in toks.items():
            if s == own and (e == "pe" or not self.same_engine_sync):
                continue
            if self.seen[e].get(s, 0) < v:
                self.seen[e][s] = v
                self.lists[e].append(("wait", s, v))

    def _commit(self, tok, reads, writes):
        for b in writes:
            self.last_write[b] = tok
            self.readers[b] = []
        for b in reads:
            self.readers.setdefault(b, []).append(tok)

    def op(self, e, fn, reads=(), writes=()):
        self._deps(e, reads, writes)
        s = "E_" + e
        self.cnt[s] += 1
        tok = (s, self.cnt[s])
        self.lists[e].append(("op", fn, s, 1))
        self._commit(tok, reads, writes)
        return tok

    def dma(self, q, out, in_, slot, reads=(), writes=(), is_output=False):
        s = self._sem("D_" + str(slot))
        self._deps(q, reads, writes, own_dma_sem=s)
        self.cnt[s] += 16
        tok = (s, self.cnt[s])
        self.lists[q].append(("op", lambda eng, o=out, i=in_: eng.dma_start(out=o, in_=i), s, 16))
        self._commit(tok, reads, writes)
        if is_output:
            self.out_tokens.append(tok)
        return tok

    def barrier(self):
        for e in self.ENGS:
            for s in self.sem_names:
                v = self.cnt[s]
                if v > 0 and self.seen[e].get(s, 0) < v:
                    self.seen[e][s] = v
                    self.lists[e].append(("wait", s, v))

    def finish(self):
        self.barrier()

    def emit(self):
        nc = self.nc
        with ExitStack() as st:
            sems = {}
            for i, name in enumerate(self.sem_names):
                sems[name] = st.enter_context(nc.semaphore("s%d" % i))
            block = st.enter_context(nc.Block())
            lists = self.lists

            def replay(eng, items):
                for it in items:
                    if it[0] == "wait":
                        eng.wait_ge(sems[it[1]], it[2])
                    else:
                        it[1](eng).then_inc(sems[it[2]], it[3])

            @block.tensor
            def _(eng):
                replay(eng, lists["pe"])

            @block.scalar
            def _(eng):
                replay(eng, lists["act"])

            @block.vector
            def _(eng):
                replay(eng, lists["dve"])

            @block.gpsimd
            def _(eng):
                replay(eng, lists["pool"])

            @block.sync
            def _(eng):
                replay(eng, lists["sp"])


class Arena:
    def __init__(self, tile, size):
        self.tile = tile
        self.size = size
        self.off = 0

    def reset(self):
        self.off = 0

    def f32(self, n):
        assert self.off + n <= self.size, ("arena overflow", self.off, n, self.size)
        ap = self.tile[:, self.off:self.off + n]
        self.off += n
        return ap

    def bf16(self, n):
        assert n % 2 == 0
        return self.f32(n // 2).bitcast(BF16)


def build(NT, NL=1, stage="B", dbg=()):
    T = NT * 128
    TB = min(512, T)
    NTB = T // TB
    TPB = TB // 128
    nc = bass.Bass("TRN2", target_bir_lowering=False)
    S = Sched(nc)
    DT = lambda name, shape, dt, kind="Internal": nc.dram_tensor(name, shape, dt, kind=kind).ap()
    IN = lambda name, shape, dt=F32: DT(name, shape, dt, "ExternalInput")
    OUT = lambda name, shape, dt=F32: DT(name, shape, dt, "ExternalOutput")

    x_in = IN("x", [T, D])
    xh_in = IN("xh", [4, D])
    pos_in = IN("pos", [128, NT], I32)
    cmask_in = IN("cmask", [128, NCORES])
    consts_in = IN("consts", [128, 128 * 4 + 64 + 18])
    A_SET = ("w_in", "cw", "cb", "gbias", "s5p", "s5rep", "s5B")
    SHAPES = dict(
        p=[T, PLE], w_in=[D, INW], cw=[128, 12, 4], cb=[128, 12], gbias=[128, 12], hnw=[128, NH * 128],
        s5p=[128, 3, 16], s5rep=[128, 3, 2048], s5B=[2, 128, 16, 128], s5C=[2, 128, 16, 128], s5D=[128, 4],
        glu_w=[512, 512], glu_b=[128, 4], w_out=[D, D], lnp=[4, 128, D], w_up=[D, DFF], w_down=[DFF, D],
        w_gate=[D, D], w_ple=[PLE, D], g_S=[NCORES, 128, NH * VW], g_A=[NCORES, 128, NH], g_X=[NCORES, 128, 32])
    L = []
    for l in range(NL):
        dd = {}
        for nm, shp in SHAPES.items():
            if stage == "A" and nm not in A_SET:
                continue
            if stage == "F" and nm.startswith("g_"):
                continue
            dd[nm] = IN(nm + str(l), shp)
        L.append(dd)
    outs = {}
    if stage == "A":
        outs["sum_S"] = OUT("sum_S", [128, NH * VW])
        outs["sum_A"] = OUT("sum_A", [128, NH])
        outs["sum_X"] = OUT("sum_X", [128, 32])
    else:
        outs["y"] = OUT("y", [T, D])
    dbg_out = {}
    SUMW = NH * VW + NH + 32
    if stage == "F":
        oh_in = IN("oh", [128, NCORES])
        ccsrc = [DT("ccsrc%d" % l, [128, SUMW], F32) for l in range(NL)]
        ccdst = [DT("ccdst%d" % l, [NCORES * 128, SUMW], F32) for l in range(NL)]
        hsrc = DT("hsrc", [4, D], F32)
        hdst = DT("hdst", [NCORES * 4, D], F32)
        halo1 = DT("halo1", [4, D], F32)

    def gsrc(l, nm, j):
        if stage == "F":
            v = ccdst[l].rearrange("(r p) n -> r p n", p=128)[j]
            if nm == "g_S":
                return v[:, 0:NH * VW]
            if nm == "g_A":
                return v[:, NH * VW:NH * VW + NH]
            return v[:, NH * VW + NH:SUMW]
        return L[l][nm][j]

    QK = DT("QK", [NT, 128, 2, NH, 128], BF16)
    VV = DT("VV", [NT, 128, NH, VW], BF16)
    GT = DT("GT", [NT, 128, NH * 128], F32)
    UT = DT("UT", [4, 128, T], F32)
    R1 = DT("R1", [NT, 128, D], F32)
    X1 = DT("X1", [NT, 128, D], F32)
    HH = DT("HH", [NT, 128, 64, 128], BF16)
    R2 = DT("R2", [NT, 128, D], F32)
    R3 = DT("R3", [NT, 128, D], F32)
    XB = DT("XB", [NT, 128, D], F32)
    W16D = DT("W16D", [NTB, 16, 128, 2 * TB], BF16)

    with ExitStack() as st:
        TL = lambda name, shape, dt: st.enter_context(nc.sbuf_tensor(name, shape, dt))
        PS = lambda name, shape, dt: st.enter_context(nc.psum_tensor(name, shape, dt))
        AT = TL("AT", [128, 16, T], BF16)
        ATP = TL("ATP", [128, 2, T], BF16)
        ATH = TL("ATH", [128, 16, 4], BF16)
        CST = TL("CST", [128, 128 * 4 + 64 + 18], F32)
        identb = TL("identb", [128, 128], BF16)
        maskb = TL("maskb", [128, 128], BF16)
        QS = TL("QS", [128, NT, NH], F32)
        KS = TL("KS", [128, NT, NH], F32)
        AA = TL("AA", [128, NT, NH], F32)
        COS = TL("COS", [128, NT, 64], F32)
        SIN = TL("SIN", [128, NT, 64], F32)
        POSI = TL("POSI", [128, NT], I32)
        POSF = TL("POSF", [128, NT], F32)
        CMASK = TL("CMASK", [128, NCORES], F32)
        SMALL = TL("SMALL", [128, 1280], F32)
        ARSZ = 28928
        ARENA = TL("ARENA", [128, ARSZ], F32)
        ar = Arena(ARENA, ARSZ)
        ps = [PS("ps%d" % i, [128, 512], F32) for i in range(7)]
        psb = PS("psb", [128, 1024], BF16)

        identf = CST[:, 0:128]
        tri = CST[:, 128:256]
        onesf = CST[:, 256:384]
        invf = CST[:, 512:576]
        retc = CST[:, 576:594]

        sm_off = [0]

        def sm(n):
            ap = SMALL[:, sm_off[0]:sm_off[0] + n]
            sm_off[0] += n
            assert sm_off[0] <= 1280
            return ap

        def OP(e, method, *args, reads=(), writes=(), **kw):
            return S.op(e, lambda eng, m=method, a=args, k=kw: getattr(eng, m)(*a, **k), reads, writes)

        S.dma("sp", CST[:], consts_in, "CST", writes=["CST"])
        S.dma("sp", POSI[:], pos_in, "POSI", writes=["POSI"])
        S.dma("sp", CMASK[:], cmask_in, "CMASK", writes=["CMASK"])
        OP("dve", "tensor_copy", identb[:], identf, reads=["CST"], writes=["identb"])
        OP("dve", "tensor_copy", maskb[:], tri, reads=["CST"], writes=["maskb"])
        OP("dve", "tensor_copy", POSF[:], POSI[:], reads=["POSI"], writes=["POSF"])

        def range_reduce_sin(dst, ang, tmp_i, tmp_f, keys, n):
            OP("dve", "tensor_scalar", tmp_f, ang, 1.0 / TWO_PI, None, ALU.mult, reads=keys, writes=keys)
            OP("dve", "tensor_copy", tmp_i, tmp_f, reads=keys, writes=keys)
            OP("dve", "tensor_copy", tmp_f, tmp_i, reads=keys, writes=keys)
            OP("dve", "scalar_tensor_tensor", ang, tmp_f, -CW1, ang, ALU.mult, ALU.add, reads=keys, writes=keys)
            OP("dve", "scalar_tensor_tensor", ang, tmp_f, -CW2, ang, ALU.mult, ALU.add, reads=keys, writes=keys)
            OP("dve", "tensor_scalar", ang, ang, math.pi, -math.pi, ALU.min, ALU.max, reads=keys, writes=keys)
            OP("act", "activation", dst, ang, AF.Sin, reads=keys, writes=keys)

        def cos_from_reduced(dst, red, tmp, tmp2, keys):
            OP("dve", "tensor_scalar", tmp, red, math.pi / 2, None, ALU.add, reads=keys, writes=keys)
            OP("dve", "tensor_scalar", tmp2, tmp, math.pi, -TWO_PI, ALU.is_gt, ALU.mult, reads=keys, writes=keys)
            OP("dve", "tensor_tensor", tmp, tmp, tmp2, ALU.add, reads=keys, writes=keys)
            OP("dve", "tensor_scalar", tmp, tmp, math.pi, -math.pi, ALU.min, ALU.max, reads=keys, writes=keys)
            OP("act", "activation", dst, tmp, AF.Sin, reads=keys, writes=keys)

        ar.reset()
        ang = ar.f32(NT * 64)
        ang2 = ar.f32(NT * 64)
        tf = ar.f32(NT * 64)
        ti = ar.f32(NT * 64).bitcast(I32)
        kk = ["rot"]
        for tt in range(NT):
            OP("dve", "tensor_scalar", ang[:, tt * 64:(tt + 1) * 64], invf, POSF[:, tt:tt + 1], None, ALU.mult,
               reads=["CST", "POSF"] + kk, writes=kk)
        range_reduce_sin(SIN[:].rearrange("p a b -> p (a b)"), ang, ti, tf, kk + ["SIN"], NT * 64)
        cos_from_reduced(COS[:].rearrange("p a b -> p (a b)"), ang, ang2, tf, kk + ["COS"])

        if "rot" in dbg:
            dbg_out["d_cos"] = OUT("d_cos", [128, NT, 64])
            dbg_out["d_sin"] = OUT("d_sin", [128, NT, 64])
            S.dma("sp", dbg_out["d_cos"], COS[:], "dbgc", reads=["COS"], is_output=True)
            S.dma("sp", dbg_out["d_sin"], SIN[:], "dbgs", reads=["SIN"], is_output=True)

        def load_AT_from(src_tiles, src_halo=None):
            S.barrier()
            ar.reset()
            xs = [ar.f32(D), ar.f32(D)]
            for tt in range(NT):
                sl = tt % 2
                S.dma("sp", xs[sl], src_tiles(tt), "xs%d" % sl, writes=["xs%d" % sl])
                for g in range(4):
                    b = ps[g]
                    for j in range(4):
                        kc = 4 * g + j
                        OP("pe", "transpose", b[:, j * 128:(j + 1) * 128], xs[sl][:, kc * 128:(kc + 1) * 128], identf,
                           reads=["xs%d" % sl, "CST"], writes=["ps%d" % g])
                    eng = "act" if g % 2 == 0 else "dve"
                    dst = AT[:, 4 * g:4 * g + 4, tt * 128:(tt + 1) * 128]
                    srcv = b[:].rearrange("p (a b) -> p a b", a=4)
                    if eng == "act":
                        OP("act", "activation", dst, srcv, AF.Copy, reads=["ps%d" % g], writes=["AT"])
                    else:
                        OP("dve", "tensor_copy", dst, srcv, reads=["ps%d" % g], writes=["AT"])
            if src_halo is not None:
                xh = ar.f32(D)
                S.dma("sp", xh[0:4, :], src_halo, "xh", writes=["xh"])
                for kc in range(16):
                    OP("pe", "transpose", ps[4][:, kc * 4:(kc + 1) * 4], xh[0:4, kc * 128:(kc + 1) * 128], identf[0:4, 0:4],
                       reads=["xh", "CST"], writes=["ps4"])
                OP("dve", "tensor_copy", ATH[:], ps[4][:, 0:64].rearrange("p (a b) -> p a b", a=16), reads=["ps4"], writes=["ATH"])

        def load_w(wslots, gi, w_dram, c0, width, KC, key="wt"):
            sl = gi % len(wslots)
            dst = wslots[sl][:, 0:KC * width].rearrange("p (k n) -> p k n", k=KC)
            src = w_dram.rearrange("(k p) n -> p k n", p=128)[:, :, c0:c0 + width]
            nsp = 4 if KC >= 4 else 1
            step = KC // nsp
            for i in range(nsp):
                S.dma("pool", dst[:, i * step:(i + 1) * step, :], src[:, i * step:(i + 1) * step, :], "%s%d" % (key, sl),
                      writes=["%s%d" % (key, sl)])
            return dst, "%s%d" % (key, sl)

        def layer(l, x_tiles, x_halo, out_tiles, out_is_ext):
            P = L[l]
            load_AT_from(x_tiles, x_halo)
            S.barrier()
            ar.reset()
            sm_off[0] = 0
            wsl = [ar.bf16(16 * 512), ar.bf16(16 * 512)]
            gbias = sm(12)
            S.dma("sp", gbias, P["gbias"], "gbias", writes=["gbias"])
            cwt = sm(48)
            cbt = sm(12)
            S.dma("sp", cwt, P["cw"].rearrange("p a b -> p (a b)"), "cwt", writes=["cwt"])
            S.dma("sp", cbt, P["cb"], "cbt", writes=["cbt"])
            for (tab, c) in ((QS, 0), (KS, 6), (AA, 12)):
                OP("dve", "tensor_copy", tab[:, :, 0:6], retc[:, c:c + 6].unsqueeze(1).to_broadcast([128, NT, 6]),
                   reads=["CST"], writes=["tabs"])

            gi = 0
            wv, wkey = load_w(wsl, gi, P["w_in"], C_MI, 12, 16)
            gt = [sm(64), sm(64)]
            for tt in range(NT):
                pa = ps[tt % 2]
                pk = "ps%d" % (tt % 2)
                for kc in range(16):
                    OP("pe", "matmul", pa[:, 0:12], AT[:, kc, tt * 128:(tt + 1) * 128], wv[:, kc, :], start=(kc == 0), stop=(kc == 15),
                       reads=["AT", wkey], writes=[pk])
                g = gt[tt % 2]
                gk = "gt%d" % (tt % 2)
                tif = g[:, 0:12]
                ee = g[:, 12:18]
                spv = g[:, 18:24]
                tmp = g[:, 24:30]
                OP("dve", "tensor_tensor", tif, pa[:, 0:12], gbias, ALU.add, reads=[pk, "gbias"], writes=[gk])
                OP("act", "activation", ee, tif[:, 6:12], AF.Exp, scale=-1.0, reads=[gk], writes=[gk])
                OP("act", "activation", spv, ee, AF.Ln, bias=1.0, reads=[gk], writes=[gk])
                pb = ps[2 + tt % 2]
                pbk = "ps%d" % (2 + tt % 2)
                OP("pe", "matmul", pb[:, 0:6], tri, spv, start=True, stop=True, reads=["CST", gk], writes=[pbk])
                OP("pe", "matmul", pb[:, 6:12], onesf, spv, start=True, stop=True, reads=["CST", gk], writes=[pbk])
                OP("act", "activation", QS[:, tt, 6:12], pb[:, 0:6], AF.Exp, scale=-1.0, reads=[pbk], writes=["tabs"])
                OP("dve", "tensor_tensor", tmp, pb[:, 0:6], tif[:, 0:6], ALU.add, reads=[pbk, gk], writes=[gk])
                OP("act", "activation", KS[:, tt, 6:12], tmp, AF.Exp, bias=-0.5 * math.log(128.0), reads=[gk], writes=["tabs"])
                OP("act", "activation", AA[:, tt, 6:12], pb[:, 6:12], AF.Exp, scale=-1.0, reads=[pbk], writes=["tabs"])
            gi += 1

            rt = [ar.f32(4 * 192), ar.f32(4 * 192)]
            ro = [ar.f32(384), ar.f32(384)]
            qstg = [ar.bf16(384), ar.bf16(384)]
            vstg = [ar.bf16(3 * VW), ar.bf16(3 * VW)]
            gstg = [ar.f32(384), ar.f32(384)]
            for sl in range(2):
                OP("pool", "memset", vstg[sl].rearrange("p (h c) -> p h c", h=3)[:, :, 128:129], 1.0, writes=["vstg%d" % sl])
                OP("pool", "memset", vstg[sl].rearrange("p (h c) -> p h c", h=3)[:, :, 129:130], 0.0, writes=["vstg%d" % sl])
            tok_groups = []
            for half in range(2):
                tok_groups.append(("q", C_RQ + 384 * half, 3 * half))
            for half in range(2):
                tok_groups.append(("k", C_RK + 384 * half, 3 * half))
            for half in range(2):
                tok_groups.append(("v", C_RV + 384 * half, 3 * half))
            for half in range(2):
                tok_groups.append(("silu", C_RG + 384 * half, 3 * half))
            for half in range(2):
                tok_groups.append(("v", C_MV + 384 * half, 6 + 3 * half))
            for half in range(2):
                tok_groups.append(("sig", C_MO + 384 * half, 6 + 3 * half))
            it = 0
            for (kind, c0, h0) in tok_groups:
                wv, wkey = load_w(wsl, gi, P["w_in"], c0, 384, 16)
                gi += 1
                for tt in range(NT):
                    b = it % 4
                    sl = it % 2
                    it += 1
                    pa = ps[b]
                    pk = "ps%d" % b
                    for kc in range(16):
                        OP("pe", "matmul", pa[:, 0:384], AT[:, kc, tt * 128:(tt + 1) * 128], wv[:, kc, :], start=(kc == 0), stop=(kc == 15),
                           reads=["AT", wkey], writes=[pk])
                    if kind in ("q", "k"):
                        pv = pa[:, 0:384].rearrange("p (h t f) -> p h t f", h=3, t=2)
                        t1 = pv[:, :, 0, :]
                        t2 = pv[:, :, 1, :]
                        cosb = COS[:, tt, :].unsqueeze(1).to_broadcast([128, 3, 64])
                        sinb = SIN[:, tt, :].unsqueeze(1).to_broadcast([128, 3, 64])
                        r = rt[sl].rearrange("p (a h f) -> p a h f", a=4, h=3)
                        rk_ = "rt%d" % sl
                        OP("dve", "tensor_tensor", r[:, 0], t1, cosb, ALU.mult, reads=[pk, "COS"], writes=[rk_])
                        OP("dve", "tensor_tensor", r[:, 1], t2, sinb, ALU.mult, reads=[pk, "SIN"], writes=[rk_])
                        OP("dve", "tensor_tensor", r[:, 2], t1, sinb, ALU.mult, reads=[pk, "SIN"], writes=[rk_])
                        OP("dve", "tensor_tensor", r[:, 3], t2, cosb, ALU.mult, reads=[pk, "COS"], writes=[rk_])
                        rov = ro[sl].rearrange("p (h t f) -> p h t f", h=3, t=2)
                        rok = "ro%d" % sl
                        OP("pool", "tensor_tensor", rov[:, :, 0, :], r[:, 0], r[:, 1], ALU.subtract, reads=[rk_], writes=[rok])
                        OP("pool", "tensor_tensor", rov[:, :, 1, :], r[:, 2], r[:, 3], ALU.add, reads=[rk_], writes=[rok])
                        tab = QS if kind == "q" else KS
                        qk_ = "qstg%d" % sl
                        for hh in range(3):
                            OP("act", "activation", qstg[sl][:, hh * 128:(hh + 1) * 128], ro[sl][:, hh * 128:(hh + 1) * 128], AF.Copy,
                               scale=tab[:, tt, h0 + hh:h0 + hh + 1], reads=[rok, "tabs"], writes=[qk_])
                        which = 0 if kind == "q" else 1
                        S.dma("sp", QK[tt, :, which, h0:h0 + 3, :], qstg[sl].rearrange("p (h d) -> p h d", h=3), qk_, reads=[qk_],
                              writes=[("QK", tt)])
                    elif kind == "v":
                        vk = "vstg%d" % sl
                        OP("act", "activation", vstg[sl].rearrange("p (h c) -> p h c", h=3)[:, :, 0:128],
                           pa[:, 0:384].rearrange("p (h c) -> p h c", h=3), AF.Copy, reads=[pk], writes=[vk])
                        S.dma("sp", VV[tt, :, h0:h0 + 3, :], vstg[sl].rearrange("p (h c) -> p h c", h=3), vk, reads=[vk], writes=[("VV", tt)])
                    else:
                        gk = "gstg%d" % sl
                        OP("act", "activation", gstg[sl], pa[:, 0:384], AF.Silu if kind == "silu" else AF.Sigmoid, reads=[pk], writes=[gk])
                        S.dma("sp", GT[tt, :, h0 * 128:(h0 + 3) * 128], gstg[sl], gk, reads=[gk], writes=[("GT", tt)])

            cin = ar.f32(T + 8)
            cacc = ar.f32(T)
            csil = ar.f32(T)
            q2 = ar.bf16(NT * 128)
            ustg = [ar.f32(TB), ar.f32(TB)]
            feat_groups = [("mq", C_MQ, 384, 0), ("mq", C_MQ + 384, 384, 3), ("mk", C_MK, 384, 0), ("mk", C_MK + 384, 384, 3),
                           ("su", C_SU, 512, 0)]
            itf = 0
            for (kind, c0, width, f0) in feat_groups:
                wv, wkey = load_w(wsl, gi, P["w_in"], c0, width, 16)
                gi += 1
                for fi in range(width // 128):
                    for tb in range(NTB):
                        b = itf % 4
                        itf += 1
                        pa = ps[b]
                        pk = "ps%d" % b
                        for kc in range(16):
                            OP("pe", "matmul", pa[:, 0:TB], wv[:, kc, fi * 128:(fi + 1) * 128], AT[:, kc, tb * TB:(tb + 1) * TB],
                               start=(kc == 0), stop=(kc == 15), reads=["AT", wkey], writes=[pk])
                        if kind == "su":
                            sl = itf % 2
                            OP("act", "activation", ustg[sl], pa[:, 0:TB], AF.Copy, reads=[pk], writes=["ustg%d" % sl])
                            S.dma("sp", UT[fi, :, tb * TB:(tb + 1) * TB], ustg[sl], "ustg%d" % sl, reads=["ustg%d" % sl], writes=["UT"])
                        else:
                            OP("act", "activation", cin[:, 4 + tb * TB:4 + (tb + 1) * TB], pa[:, 0:TB], AF.Copy, reads=[pk], writes=["cin"])
                    if kind == "su":
                        continue
                    for kc in range(16):
                        OP("pe", "matmul", ps[4][:, 0:4], wv[:, kc, fi * 128:(fi + 1) * 128], ATH[:, kc, :], start=(kc == 0), stop=(kc == 15),
                           reads=["ATH", wkey], writes=["ps4"])
                    OP("act", "activation", cin[:, 0:4], ps[4][:, 0:4], AF.Copy, reads=["ps4"], writes=["cin"])
                    h = f0 + fi
                    fidx = h if kind == "mq" else 6 + h
                    OP("dve", "tensor_scalar", cacc, cin[:, 1:1 + T], cwt[:, fidx * 4:fidx * 4 + 1], cbt[:, fidx:fidx + 1], ALU.mult, ALU.add,
                       reads=["cin", "cwt", "cbt"], writes=["cacc"])
                    for tap in range(1, 4):
                        OP("dve", "scalar_tensor_tensor", cacc, cin[:, 1 + tap:1 + tap + T], cwt[:, fidx * 4 + tap:fidx * 4 + tap + 1], cacc,
                           ALU.mult, ALU.add, reads=["cin", "cwt", "cacc"], writes=["cacc"])
                    OP("act", "activation", csil, cacc, AF.Silu, reads=["cacc"], writes=["csil"])
                    tab = QS if kind == "mq" else KS
                    for tt in range(NT):
                        b = 5 + tt % 2
                        OP("pe", "transpose", ps[b][:, 0:128], csil[:, tt * 128:(tt + 1) * 128], identf, reads=["csil", "CST"], writes=["ps%d" % b])
                        OP("act", "activation", q2[:, tt * 128:(tt + 1) * 128], ps[b][:, 0:128], AF.Copy, scale=tab[:, tt, 6 + h:7 + h],
                           reads=["ps%d" % b, "tabs"], writes=["q2"])
                    which = 0 if kind == "mq" else 1
                    S.dma("sp", QK[:, :, which, 6 + h, :].rearrange("t p d -> p t d"), q2.rearrange("p (t d) -> p t d", t=NT), "q2",
                          reads=["q2"], writes=[("QK", tt) for tt in range(NT)])

            if "p1" in dbg:
                S.barrier()
                for nm, src, shp, dt_ in (("d_QK", QK, [NT, 128, 2, NH, 128], BF16), ("d_VV", VV, [NT, 128, NH, VW], BF16),
                                          ("d_GT", GT, [NT, 128, NH * 128], F32), ("d_UT", UT, [4, 128, T], F32)):
                    dbg_out[nm] = OUT(nm, shp, dt_)
                    S.dma("sp", dbg_out[nm], src, nm, is_output=True)
                tabs_o = OUT("d_tabs", [128, 3, NT, NH])
                dbg_out["d_tabs"] = tabs_o
                for i, tb_ in enumerate((QS, KS, AA)):
                    S.dma("sp", tabs_o[:, i], tb_[:], "dtab%d" % i, reads=["tabs"], is_output=True)
                return


            def pm(full):
                CUT = 99
                S.barrier()
                ar.reset()
                sm_off[0] = 0
                Sst = ar.f32(NH * VW)
                Sbf = ar.bf16(NH * VW)
                Sv = Sst.rearrange("p (h c) -> p h c", h=NH)
                Sbv = Sbf.rearrange("p (h c) -> p h c", h=NH)
                qksl = [ar.bf16(2 * NH * 128), ar.bf16(2 * NH * 128)]
                vsl = [ar.bf16(NH * VW), ar.bf16(NH * VW)]
                OP("dve", "memset", Sst, 0.0, writes=["Sst"])
                if full:
                    gS = [ar.f32(NH * VW), ar.f32(NH * VW)]
                    gA = [sm(NH), sm(NH)]
                    aeff = sm(NH)
                    for j in range(NCORES):
                        sl = j % 2
                        S.dma("sp", gS[sl], gsrc(l, "g_S", j), "gS%d" % sl, reads=[("ccdst", l)], writes=["gS%d" % sl])
                        S.dma("sp", gA[sl], gsrc(l, "g_A", j), "gA%d" % sl, reads=[("ccdst", l)], writes=["gA%d" % sl])
                        OP("dve", "tensor_scalar", aeff, gA[sl], -1.0, CMASK[:, j:j + 1], ALU.add, ALU.mult, reads=["gA%d" % sl, "CMASK"], writes=["aeff"])
                        OP("dve", "tensor_scalar", aeff, aeff, 1.0, None, ALU.add, reads=["aeff"], writes=["aeff"])
                        OP("dve", "tensor_scalar", gS[sl], gS[sl], CMASK[:, j:j + 1], None, ALU.mult, reads=["gS%d" % sl, "CMASK"], writes=["gS%d" % sl])
                        gv = gS[sl].rearrange("p (h c) -> p h c", h=NH)
                        for h in range(NH):
                            OP("dve", "scalar_tensor_tensor", Sv[:, h, :], Sv[:, h, :], aeff[:, h:h + 1], gv[:, h, :], ALU.mult, ALU.add,
                               reads=["Sst", "aeff", "gS%d" % sl], writes=["Sst"])
                OP("dve", "tensor_copy", Sbf, Sst, reads=["Sst"], writes=["Sbf"])
                atot = sm(NH)
                OP("dve", "memset", atot, 1.0, writes=["atot"])
                if full:
                    gsl = [ar.f32(NH * 128), ar.f32(NH * 128)]
                    hnw = ar.f32(NH * 128)
                    S.dma("sp", hnw, P["hnw"], "hnw", writes=["hnw"])
                    qkT = ar.bf16(24 * 128)
                    qkTv = qkT.rearrange("p (a d) -> p a d", a=24)
                    sT = ar.bf16(6 * 128)
                    sTv = sT.rearrange("p (h d) -> p h d", h=6)
                    yn = ar.f32(6 * 128)
                    ybf = ar.bf16(6 * 128)
                    gw = ar.f32(NH * 128)
                    st6 = [sm(36), sm(36)]
                    mv2 = [sm(12), sm(12)]
                    sc6 = [sm(24), sm(24)]
                    ps6b = ps[6][:].bitcast(BF16)

                def load_chunk(tt):
                    sl = tt % 2
                    S.dma("sp", qksl[sl], QK[tt].rearrange("p a h d -> p (a h d)"), "qksl%d" % sl, reads=[("QK", tt)], writes=["qksl%d" % sl])
                    S.dma("sp", vsl[sl], VV[tt].rearrange("p h c -> p (h c)"), "vsl%d" % sl, reads=[("VV", tt)], writes=["vsl%d" % sl])
                    if full:
                        S.dma("sp", gsl[sl], GT[tt], "gsl%d" % sl, reads=[("GT", tt)], writes=["gsl%d" % sl])

                load_chunk(0)
                for tt in range(NT):
                    sl = tt % 2
                    if tt + 1 < NT:
                        load_chunk(tt + 1)
                    qkv = qksl[sl].rearrange("p (a h d) -> p a h d", a=2, h=NH)
                    vv = vsl[sl].rearrange("p (h c) -> p h c", h=NH)
                    qkk, vk, gk_ = "qksl%d" % sl, "vsl%d" % sl, "gsl%d" % sl
                    if full:
                        for r in range(3):
                            for j in range(8):
                                idx = r * 8 + j
                                OP("pe", "transpose", psb[:, j * 128:(j + 1) * 128], qkv[:, idx // NH, idx % NH, :], identb[:],
                                   reads=[qkk, "identb"], writes=["psb"])
                            if r % 2 == 0:
                                OP("act", "activation", qkT[:, r * 1024:(r + 1) * 1024], psb[:], AF.Copy, reads=["psb"], writes=["qkT"])
                            else:
                                OP("dve", "tensor_copy", qkT[:, r * 1024:(r + 1) * 1024], psb[:], reads=["psb"], writes=["qkT"])
                        OP("pool", "tensor_tensor", gw, gsl[sl], hnw, ALU.mult, reads=[gk_, "hnw"], writes=["gw"])
                    for half in range(2):
                        h0 = 6 * half
                        if full and CUT >= 2:
                            for hh in range(6):
                                h = h0 + hh
                                bank = ps[0] if hh < 4 else ps[1]
                                bk = "ps0" if hh < 4 else "ps1"
                                col = (hh % 4) * 128
                                OP("pe", "matmul", bank[:, col:col + 128], qkTv[:, NH + h, :], qkTv[:, h, :], start=True, stop=True,
                                   reads=["qkT"], writes=[bk])
                            OP("dve", "tensor_tensor", sTv[:, 0:4, :], ps[0][:].rearrange("p (h d) -> p h d", h=4),
                               maskb[:].unsqueeze(1).to_broadcast([128, 4, 128]), ALU.mult, reads=["ps0", "maskb"], writes=["sT"])
                            OP("dve", "tensor_tensor", sTv[:, 4:6, :], ps[1][:, 0:256].rearrange("p (h d) -> p h d", h=2),
                               maskb[:].unsqueeze(1).to_broadcast([128, 2, 128]), ALU.mult, reads=["ps1", "maskb"], writes=["sT"])
                            for hh in range(6 if CUT >= 3 else 0):
                                h = h0 + hh
                                bank = ps[2 + hh // 3]
                                bk = "ps%d" % (2 + hh // 3)
                                col = (hh % 3) * VW
                                OP("pe", "matmul", bank[:, col:col + VW], sTv[:, hh, :], vv[:, h, :], start=True, stop=False,
                                   reads=["sT", vk], writes=[bk])
                                OP("pe", "matmul", bank[:, col:col + VW], qkTv[:, h, :], Sbv[:, h, :], start=False, stop=True,
                                   reads=["qkT", "Sbf"], writes=[bk])
                        for hh in range(6):
                            h = h0 + hh
                            bank = ps[4 + hh // 3]
                            bk = "ps%d" % (4 + hh // 3)
                            col = (hh % 3) * VW
                            OP("pe", "matmul", bank[:, col:col + VW], qkv[:, 1, h, :], vv[:, h, :], start=True, stop=True,
                               reads=[qkk, vk], writes=[bk])
                        for g3 in range(2):
                            hs = h0 + 3 * g3
                            OP("dve", "tensor_tensor", Sv[:, hs:hs + 3, :], ps[4 + g3][:, 0:3 * VW].rearrange("p (h c) -> p h c", h=3), Sv[:, hs:hs + 3, :],
                               ALU.add, reads=["ps%d" % (4 + g3), "Sst"], writes=["Sst"])
                        for hh in range(6):
                            h = h0 + hh
                            OP("act", "activation", Sv[:, h, :], Sv[:, h, :], AF.Copy, scale=AA[:, tt, h:h + 1], reads=["Sst", "tabs"], writes=["Sst"])
                        OP("pool", "tensor_copy", Sbv[:, h0:h0 + 6, :], Sv[:, h0:h0 + 6, :], reads=["Sst"], writes=["Sbf"])
                        if not full or CUT < 5:
                            continue
                        s6 = st6[half]
                        m2 = mv2[half]
                        sc = sc6[half]
                        ek = "ep%d" % half
                        for hh in range(6):
                            bank = ps[2 + hh // 3]
                            col = (hh % 3) * VW
                            OP("dve", "bn_stats", s6[:, hh * 6:(hh + 1) * 6], bank[:, col:col + 128], reads=["ps%d" % (2 + hh // 3)], writes=[ek])
                            OP("dve", "bn_aggr", m2[:, hh * 2:(hh + 1) * 2], s6[:, hh * 6:(hh + 1) * 6], reads=[ek], writes=[ek])
                        if CUT < 6:
                            continue
                        m2v = m2.rearrange("p (h c) -> p h c", c=2)
                        mean = m2v[:, :, 0]
                        var = m2v[:, :, 1]
                        rstd = sc[:, 0:6]
                        nb = sc[:, 6:12]
                        rden = sc[:, 12:18]
                        tmp6 = sc[:, 18:24]
                        if half == 1:
                            for g3 in range(2):
                                OP("act", "activation", rden[:, 3 * g3:3 * g3 + 3], ps[2 + g3][:, 0:3 * VW].rearrange("p (h c) -> p h c", h=3)[:, :, 128],
                                   AF.Abs, reads=["ps%d" % (2 + g3)], writes=[ek])
                            OP("dve", "tensor_scalar", rden, rden, 1.0, None, ALU.max, reads=[ek], writes=[ek])
                            OP("dve", "reciprocal", rden, rden, reads=[ek], writes=[ek])
                            OP("dve", "tensor_tensor", tmp6, rden, rden, ALU.mult, reads=[ek], writes=[ek])
                            OP("dve", "tensor_tensor", tmp6, tmp6, var, ALU.mult, reads=[ek], writes=[ek])
                            OP("act", "activation", rstd, tmp6, AF.Sqrt, bias=EPS, reads=[ek], writes=[ek])
                            OP("dve", "reciprocal", rstd, rstd, reads=[ek], writes=[ek])
                            OP("dve", "tensor_tensor", rstd, rstd, rden, ALU.mult, reads=[ek], writes=[ek])
                        else:
                            OP("act", "activation", rstd, var, AF.Sqrt, bias=EPS, reads=[ek], writes=[ek])
                            OP("dve", "reciprocal", rstd, rstd, reads=[ek], writes=[ek])
                        OP("dve", "scalar_tensor_tensor", nb, mean, -1.0, rstd, ALU.mult, ALU.mult, reads=[ek], writes=[ek])
                        if CUT < 7:
                            continue
                        ynv = yn.rearrange("p (h d) -> p h d", h=6)
                        for hh in range(6):
                            bank = ps[2 + hh // 3]
                            col = (hh % 3) * VW
                            OP("act", "activation", ynv[:, hh, :], bank[:, col:col + 128], AF.Identity, scale=rstd[:, hh:hh + 1], bias=nb[:, hh:hh + 1],
                               reads=["ps%d" % (2 + hh // 3), ek], writes=["yn"])
                        if CUT < 8:
                            continue
                        OP("dve", "tensor_tensor", ybf, yn, gw[:, h0 * 128:(h0 + 6) * 128], ALU.mult, reads=["yn", "gw"], writes=["ybf"])
                        for hh in range(6):
                            OP("pe", "transpose", psb[:, hh * 128:(hh + 1) * 128], ybf[:, hh * 128:(hh + 1) * 128], identb[:], reads=["ybf", "identb"],
                               writes=["psb"])
                        OP("act", "activation", AT[:, h0:h0 + 6, tt * 128:(tt + 1) * 128], psb[:, 0:768].rearrange("p (h d) -> p h d", h=6), AF.Copy,
                           reads=["psb"], writes=["AT"])
                    OP("dve", "tensor_tensor", atot, atot, AA[:, tt, :], ALU.mult, reads=["atot", "tabs"], writes=["atot"])
                if stage == "A":
                    S.dma("sp", outs["sum_S"], Sst, "Sst_o", reads=["Sst"], is_output=True)
                    S.dma("sp", outs["sum_A"], atot, "atot_o", reads=["atot"], is_output=True)
                elif stage == "F" and not full:
                    S.dma("sp", ccsrc[l][:, 0:NH * VW], Sst, "Sst_o", reads=["Sst"], writes=[("ccsrc", l)])
                    S.dma("sp", ccsrc[l][:, NH * VW:NH * VW + NH], atot, "atot_o", reads=["atot"], writes=[("ccsrc", l)])


            s5st = {}

            def s5_a():
                S.barrier()
                ar.reset()
                sm_off[0] = 0
                KP = 7 + int(round(math.log2(NT)))
                bb16 = ar.bf16(4096)
                bb_re = bb16[:, 0:2048].rearrange("p (t n) -> p t n", t=16)
                bb_im = bb16[:, 2048:4096].rearrange("p (t n) -> p t n", t=16)
                mark_bb = ar.off
                bbT = ar.f32(4096)
                mark = ar.off
                dk = ["disc"]

                def disc(lre, lim, ldt, n, alloc):
                    dt_ = alloc(n)
                    OP("act", "activation", dt_, ldt, AF.Exp, reads=dk, writes=dk)
                    a = alloc(n)
                    OP("dve", "tensor_tensor", a, lre, dt_, ALU.mult, reads=dk, writes=dk)
                    mag = alloc(n)
                    OP("act", "activation", mag, a, AF.Exp, reads=dk, writes=dk)
                    an = alloc(n)
                    OP("dve", "tensor_tensor", an, lim, dt_, ALU.mult, reads=dk, writes=dk)
                    sinv, cosv, tf_, t2_ = alloc(n), alloc(n), alloc(n), alloc(n)
                    ti_ = alloc(n).bitcast(I32)
                    range_reduce_sin(sinv, an, ti_, tf_, dk, n)
                    cos_from_reduced(cosv, an, t2_, tf_, dk)
                    OP("dve", "tensor_tensor", cosv, cosv, mag, ALU.mult, reads=dk, writes=dk)
                    OP("dve", "tensor_tensor", sinv, sinv, mag, ALU.mult, reads=dk, writes=dk)
                    OP("dve", "tensor_scalar", a, cosv, -1.0, None, ALU.add, reads=dk, writes=dk)
                    OP("dve", "tensor_tensor", dt_, lre, lre, ALU.mult, reads=dk, writes=dk)
                    OP("dve", "tensor_tensor", tf_, lim, lim, ALU.mult, reads=dk, writes=dk)
                    OP("dve", "tensor_tensor", dt_, dt_, tf_, ALU.add, reads=dk, writes=dk)
                    OP("dve", "reciprocal", dt_, dt_, reads=dk, writes=dk)
                    OP("dve", "tensor_tensor", t2_, a, lre, ALU.mult, reads=dk, writes=dk)
                    OP("dve", "tensor_tensor", tf_, sinv, lim, ALU.mult, reads=dk, writes=dk)
                    OP("dve", "tensor_tensor", t2_, t2_, tf_, ALU.add, reads=dk, writes=dk)
                    OP("dve", "tensor_tensor", mag, t2_, dt_, ALU.mult, reads=dk, writes=dk)
                    OP("dve", "tensor_tensor", tf_, sinv, lre, ALU.mult, reads=dk, writes=dk)
                    OP("dve", "tensor_tensor", an, a, lim, ALU.mult, reads=dk, writes=dk)
                    OP("dve", "tensor_tensor", tf_, tf_, an, ALU.subtract, reads=dk, writes=dk)
                    OP("dve", "tensor_tensor", an, tf_, dt_, ALU.mult, reads=dk, writes=dk)
                    return cosv, sinv, mag, an

                for hf in range(2):
                    S.barrier()
                    ar.off = mark
                    c0_ = hf * 1024
                    rep = ar.f32(3 * 1024)
                    S.dma("sp", rep.rearrange("p (a n) -> p a n", a=3), P["s5rep"][:, :, c0_:c0_ + 1024], "s5rep", writes=dk)
                    Bre = ar.f32(1024)
                    Bim = ar.f32(1024)
                    S.dma("sp", Bre, P["s5B"][0].rearrange("p t n -> p (t n)")[:, c0_:c0_ + 1024], "Bre", writes=["Bblk"])
                    S.dma("sp", Bim, P["s5B"][1].rearrange("p t n -> p (t n)")[:, c0_:c0_ + 1024], "Bim", writes=["Bblk"])
                    _, _, wre, wim = disc(rep[:, 0:1024], rep[:, 1024:2048], rep[:, 2048:3072], 1024, ar.f32)
                    t1_ = ar.f32(1024)
                    o_re = bbT[:, c0_:c0_ + 1024]
                    o_im = bbT[:, 2048 + c0_:2048 + c0_ + 1024]
                    OP("dve", "tensor_tensor", o_re, wre, Bre, ALU.mult, reads=dk + ["Bblk"], writes=["bbT"])
                    OP("dve", "tensor_tensor", t1_, wim, Bim, ALU.mult, reads=dk + ["Bblk"], writes=["t1_"])
                    OP("dve", "tensor_tensor", o_re, o_re, t1_, ALU.subtract, reads=["bbT", "t1_"], writes=["bbT"])
                    OP("dve", "tensor_tensor", o_im, wre, Bim, ALU.mult, reads=dk + ["Bblk"], writes=["bbT"])
                    OP("dve", "tensor_tensor", t1_, wim, Bre, ALU.mult, reads=dk + ["Bblk", "t1_"], writes=["t1_"])
                    OP("dve", "tensor_tensor", o_im, o_im, t1_, ALU.add, reads=["bbT", "t1_"], writes=["bbT"])
                OP("dve", "tensor_copy", bb16, bbT, reads=["bbT"], writes=["bb16"])
                S.barrier()
                ar.off = mark_bb
                Em_re = ar.f32(2048).rearrange("p (t n) -> p t n", t=16)
                Em_im = ar.f32(2048).rearrange("p (t n) -> p t n", t=16)
                Ep_re = ar.f32(2048).rearrange("p (t n) -> p t n", t=16)
                nEp_im = ar.f32(2048).rearrange("p (t n) -> p t n", t=16)
                p16 = sm(48)
                S.dma("sp", p16, P["s5p"].rearrange("p a t -> p (a t)"), "s5p", writes=dk)
                lbr, lbi, _, _ = disc(p16[:, 0:16], p16[:, 16:32], p16[:, 32:48], 16, sm)
                pwr = [lbr] + [sm(16) for _ in range(KP)]
                pwi = [lbi] + [sm(16) for _ in range(KP)]
                tq = sm(16)

                def csq(dr, di, sr, si):
                    OP("dve", "tensor_tensor", dr, sr, sr, ALU.mult, reads=dk, writes=dk)
                    OP("dve", "tensor_tensor", tq, si, si, ALU.mult, reads=dk, writes=dk)
                    OP("dve", "tensor_tensor", dr, dr, tq, ALU.subtract, reads=dk, writes=dk)
                    OP("dve", "scalar_tensor_tensor", di, sr, 2.0, si, ALU.mult, ALU.mult, reads=dk, writes=dk)

                for k in range(KP):
                    csq(pwr[k + 1], pwi[k + 1], pwr[k], pwi[k])
                mur = [sm(16) for _ in range(7)]
                mui = [sm(16) for _ in range(7)]
                OP("dve", "tensor_tensor", tq, lbr, lbr, ALU.mult, reads=dk, writes=dk)
                OP("dve", "tensor_tensor", mur[0], lbi, lbi, ALU.mult, reads=dk, writes=dk)
                OP("dve", "tensor_tensor", tq, tq, mur[0], ALU.add, reads=dk, writes=dk)
                OP("dve", "reciprocal", tq, tq, reads=dk, writes=dk)
                OP("dve", "tensor_tensor", mur[0], lbr, tq, ALU.mult, reads=dk, writes=dk)
                OP("dve", "scalar_tensor_tensor", mui[0], lbi, -1.0, tq, ALU.mult, ALU.mult, reads=dk, writes=dk)
                for k in range(6):
                    csq(mur[k + 1], mui[k + 1], mur[k], mui[k])
                mark2 = ar.off
                et1 = ar.f32(16 * 64).rearrange("p (t n) -> p t n", t=16)
                et2 = ar.f32(16 * 64).rearrange("p (t n) -> p t n", t=16)

                def build_table(Er, Ei, pr, pi):
                    OP("dve", "memset", Er[:, :, 0:1], 1.0, reads=dk, writes=dk)
                    OP("dve", "memset", Ei[:, :, 0:1], 0.0, reads=dk, writes=dk)
                    for k in range(7):
                        n = 1 << k
                        brd = lambda v: v.unsqueeze(2).to_broadcast([128, 16, n])
                        a1, a2 = et1[:, :, 0:n], et2[:, :, 0:n]
                        OP("dve", "tensor_tensor", a1, Er[:, :, 0:n], brd(pr[k]), ALU.mult, reads=dk, writes=dk)
                        OP("dve", "tensor_tensor", a2, Ei[:, :, 0:n], brd(pi[k]), ALU.mult, reads=dk, writes=dk)
                        OP("dve", "tensor_tensor", Er[:, :, n:2 * n], a1, a2, ALU.subtract, reads=dk, writes=dk)
                        OP("dve", "tensor_tensor", a1, Er[:, :, 0:n], brd(pi[k]), ALU.mult, reads=dk, writes=dk)
                        OP("dve", "tensor_tensor", a2, Ei[:, :, 0:n], brd(pr[k]), ALU.mult, reads=dk, writes=dk)
                        OP("dve", "tensor_tensor", Ei[:, :, n:2 * n], a1, a2, ALU.add, reads=dk, writes=dk)

                build_table(Ep_re, nEp_im, pwr, pwi)
                OP("dve", "tensor_scalar", nEp_im, nEp_im, -1.0, None, ALU.mult, reads=dk, writes=dk)
                build_table(Em_re, Em_im, mur, mui)
                L_re, L_im = pwr[7], pwi[7]
                S.barrier()
                ar.off = mark2
                s5st["W16D"] = W16D
                WL = ar.f32(2 * 16 * NT)
                WLv = WL.rearrange("p (k t c) -> p k t c", k=2, t=16)
                CIN = ar.f32(2 * 16 * NT)
                CINv = CIN.rearrange("p (k t c) -> p k t c", k=2, t=16)
                CAR = sm(32)
                usl = [ar.f32(4 * TB), ar.f32(4 * TB)]
                u16 = ar.bf16(4 * TB)
                rmask = ar.f32(TB)
                OP("pool", "memset", rmask, 1.0, writes=["rmask"])
                OP("pool", "memset", rmask.rearrange("p (c n) -> p c n", n=128)[:, :, 0:1], 0.0, reads=["rmask"], writes=["rmask"])
                mark3 = ar.off
                s5st.update(dict(Em_re=Em_re, Em_im=Em_im, Ep_re=Ep_re, nEp_im=nEp_im, L_re=L_re, L_im=L_im, pwr=pwr, pwi=pwi, KP=KP,
                                 WLv=WLv, CINv=CINv, CAR=CAR, usl=usl, u16=u16, mark3=mark3, dk=dk))
                sets = []
                for i in range(2):
                    sets.append(dict(zt1=ar.f32(TB), zt2=ar.f32(TB), zz=ar.f32(2 * TB), ww=ar.f32(2 * TB), w16=ar.bf16(2 * TB)))
                bc4 = lambda v: v.unsqueeze(1).to_broadcast([128, TPB, 128])
                v4 = lambda v: v.rearrange("p (c n) -> p c n", n=128)
                u16v = u16.rearrange("p (c n) -> p c n", c=4)
                for tb in range(NTB):
                    us = usl[tb % 2]
                    uk = "usl%d" % (tb % 2)
                    usv = us.rearrange("p (c n) -> p c n", c=4)
                    S.dma("sp", usv, UT[:, :, tb * TB:(tb + 1) * TB].rearrange("c p n -> p c n"), uk, reads=["UT"], writes=[uk])
                    OP("act", "activation", u16, us, AF.Copy, reads=[uk, "u16"], writes=["u16"])
                    for t in range(16):
                        ct = t // 4
                        W = sets[t % 2]
                        sk = "set%d" % (t % 2)
                        zt1, zt2, zz, ww = W["zt1"], W["zt2"], W["zz"], W["ww"]
                        pr_, pi_ = ps[2 * (t % 2)], ps[2 * (t % 2) + 1]
                        prk, pik = "ps%d" % (2 * (t % 2)), "ps%d" % (2 * (t % 2) + 1)
                        OP("pe", "matmul", pr_[:, 0:TB], bb_re[:, t, :], u16v[:, ct, :], start=True, stop=True, reads=["bb16", "u16"], writes=[prk])
                        OP("pe", "matmul", pi_[:, 0:TB], bb_im[:, t, :], u16v[:, ct, :], start=True, stop=True, reads=["bb16", "u16"], writes=[pik])
                        OP("dve", "tensor_tensor", v4(zt1), v4(pr_[:, 0:TB]), bc4(Em_re[:, t, :]), ALU.mult, reads=[prk, sk] + dk, writes=[sk])
                        OP("dve", "tensor_tensor", v4(zt2), v4(pi_[:, 0:TB]), bc4(Em_im[:, t, :]), ALU.mult, reads=[pik, sk] + dk, writes=[sk])
                        OP("pool", "tensor_tensor", zz[:, 0:TB], zt1, zt2, ALU.subtract, reads=[sk], writes=[sk])
                        OP("dve", "tensor_tensor", v4(zt1), v4(pr_[:, 0:TB]), bc4(Em_im[:, t, :]), ALU.mult, reads=[prk, sk] + dk, writes=[sk])
                        OP("dve", "tensor_tensor", v4(zt2), v4(pi_[:, 0:TB]), bc4(Em_re[:, t, :]), ALU.mult, reads=[pik, sk] + dk, writes=[sk])
                        OP("pool", "tensor_tensor", zz[:, TB:2 * TB], zt1, zt2, ALU.add, reads=[sk], writes=[sk])
                        OP("dve", "tensor_tensor_scan", ww[:, 0:TB], rmask, zz[:, 0:TB], 0.0, ALU.mult, ALU.add, reads=[sk, "rmask"], writes=[sk])
                        OP("dve", "tensor_tensor_scan", ww[:, TB:2 * TB], rmask, zz[:, TB:2 * TB], 0.0, ALU.mult, ALU.add, reads=[sk, "rmask"], writes=[sk])
                        OP("pool", "tensor_copy", WLv[:, :, t, tb * TPB:(tb + 1) * TPB], ww.rearrange("p (k c n) -> p k c n", k=2, n=128)[:, :, :, 127],
                           reads=[sk, "WL"], writes=["WL"])
                        OP("act", "activation", W["w16"], ww, AF.Copy, reads=[sk, "w16s%d" % (t % 2)], writes=["w16s%d" % (t % 2)])
                        S.dma("sp", W16D[tb, t], W["w16"], "w16s%d" % (t % 2), reads=["w16s%d" % (t % 2)], writes=["W16D"])
                if stage in ("A", "F"):
                    OP("dve", "memset", CAR, 0.0, writes=["CAR"])
                    s5_chain()
                    if stage == "A":
                        S.dma("sp", outs["sum_X"], CAR, "CAR_o", reads=["CAR"], is_output=True)
                    else:
                        S.dma("sp", ccsrc[l][:, NH * VW + NH:SUMW], CAR, "CAR_o", reads=["CAR"], writes=[("ccsrc", l)])

            def s5_chain():
                CAR, WLv, CINv, L_re, L_im, dk = (s5st[k] for k in ("CAR", "WLv", "CINv", "L_re", "L_im", "dk"))
                sre, sim_, q1, q2 = sm(16), sm(16), sm(16), sm(16)
                ck = ["CAR", "chain"]
                for c in range(NT):
                    OP("dve", "tensor_copy", CINv[:, :, :, c], CAR.rearrange("p (k t) -> p k t", k=2), reads=ck, writes=ck + ["CIN"])
                    OP("dve", "tensor_tensor", sre, WLv[:, 0, :, c], CAR[:, 0:16], ALU.add, reads=ck + ["WL"], writes=ck)
                    OP("dve", "tensor_tensor", sim_, WLv[:, 1, :, c], CAR[:, 16:32], ALU.add, reads=ck + ["WL"], writes=ck)
                    OP("dve", "tensor_tensor", q1, sre, L_re, ALU.mult, reads=ck + dk, writes=ck)
                    OP("dve", "tensor_tensor", q2, sim_, L_im, ALU.mult, reads=ck + dk, writes=ck)
                    OP("dve", "tensor_tensor", CAR[:, 0:16], q1, q2, ALU.subtract, reads=ck, writes=ck)
                    OP("dve", "tensor_tensor", q1, sre, L_im, ALU.mult, reads=ck + dk, writes=ck)
                    OP("dve", "tensor_tensor", q2, sim_, L_re, ALU.mult, reads=ck + dk, writes=ck)
                    OP("dve", "tensor_tensor", CAR[:, 16:32], q1, q2, ALU.add, reads=ck, writes=ck)

            def s5_b():
                W16D = s5st["W16D"]
                Ep_re, nEp_im, pwr, pwi, KP, CINv, CAR, usl, dk = (s5st[k] for k in ("Ep_re", "nEp_im", "pwr", "pwi", "KP", "CINv", "CAR", "usl", "dk"))
                S.barrier()
                ar.off = s5st["mark3"]
                OP("dve", "memset", CAR, 0.0, writes=["CAR"])
                gx = [sm(32), sm(32)]
                le = sm(32)
                cn = sm(32)
                ct1 = sm(16)
                for j in range(NCORES):
                    sl = j % 2
                    S.dma("sp", gx[sl], gsrc(l, "g_X", j), "gx%d" % sl, reads=[("ccdst", l)], writes=["gx%d" % sl])
                    ck = ["CAR", "cix"]
                    mj = CMASK[:, j:j + 1]
                    OP("dve", "tensor_scalar", le[:, 0:16], pwr[KP], -1.0, mj, ALU.add, ALU.mult, reads=dk + ck + ["CMASK"], writes=ck)
                    OP("dve", "tensor_scalar", le[:, 0:16], le[:, 0:16], 1.0, None, ALU.add, reads=ck, writes=ck)
                    OP("dve", "tensor_scalar", le[:, 16:32], pwi[KP], mj, None, ALU.mult, reads=dk + ck + ["CMASK"], writes=ck)
                    OP("dve", "tensor_scalar", gx[sl], gx[sl], mj, None, ALU.mult, reads=["gx%d" % sl, "CMASK"], writes=["gx%d" % sl])
                    OP("dve", "tensor_tensor", cn[:, 0:16], le[:, 0:16], CAR[:, 0:16], ALU.mult, reads=ck, writes=ck)
                    OP("dve", "tensor_tensor", ct1, le[:, 16:32], CAR[:, 16:32], ALU.mult, reads=ck, writes=ck)
                    OP("dve", "tensor_tensor", cn[:, 0:16], cn[:, 0:16], ct1, ALU.subtract, reads=ck, writes=ck)
                    OP("dve", "tensor_tensor", cn[:, 16:32], le[:, 0:16], CAR[:, 16:32], ALU.mult, reads=ck, writes=ck)
                    OP("dve", "tensor_tensor", ct1, le[:, 16:32], CAR[:, 0:16], ALU.mult, reads=ck, writes=ck)
                    OP("dve", "tensor_tensor", cn[:, 16:32], cn[:, 16:32], ct1, ALU.add, reads=ck, writes=ck)
                    OP("dve", "tensor_tensor", CAR, cn, gx[sl], ALU.add, reads=ck + ["gx%d" % sl], writes=ck)
                s5_chain()
                sets = []
                for i in range(2):
                    sets.append(dict(w16l=ar.bf16(2 * TB), wp=ar.f32(2 * TB), zt1=ar.f32(TB), zt2=ar.f32(TB), xx=ar.bf16(2 * TB)))
                Cblk = [ar.bf16(2 * 512), ar.bf16(2 * 512)]
                ysg = ar.f32(4 * TB)
                ysb = ar.bf16(4 * TB)
                sgb = ar.f32(TB)
                gluw = ar.bf16(4 * 512)
                gluwv = gluw.rearrange("p (k n) -> p k n", k=4)
                S.dma("pool", gluwv, P["glu_w"].rearrange("(k p) n -> p k n", p=128), "gluw", writes=["gluw"])
                s5d = sm(4)
                glub = sm(4)
                S.dma("sp", s5d, P["s5D"], "s5d", writes=["s5d"])
                S.dma("sp", glub, P["glu_b"], "glub", writes=["glub"])
                bc4 = lambda v: v.unsqueeze(1).to_broadcast([128, TPB, 128])
                v4 = lambda v: v.rearrange("p (c n) -> p c n", n=128)
                seq = [(tb, t) for tb in range(NTB) for t in range(16)]

                def ldw(i):
                    tb_, t_ = seq[i]
                    S.dma("sp", sets[i % 2]["w16l"], W16D[tb_, t_], "w16l%d" % (i % 2), reads=["W16D"], writes=["w16l%d" % (i % 2)])

                ldw(0)
                for i, (tb, t) in enumerate(seq):
                    if i + 1 < len(seq):
                        ldw(i + 1)
                    ct = t // 4
                    if t == 0:
                        us = usl[tb % 2]
                        uk = "usl%d" % (tb % 2)
                        usv = us.rearrange("p (c n) -> p c n", c=4)
                        S.dma("sp", usv, UT[:, :, tb * TB:(tb + 1) * TB].rearrange("c p n -> p c n"), uk, reads=["UT"], writes=[uk])
                    if t % 4 == 0:
                        cb_ = Cblk[ct % 2]
                        cbk = "Cblk%d" % (ct % 2)
                        for k in range(2):
                            S.dma("pool", cb_[:, k * 512:(k + 1) * 512].rearrange("p (t c) -> p t c", t=4), P["s5C"][k][:, 4 * ct:4 * ct + 4, :], cbk,
                                  writes=[cbk])
                    W = sets[i % 2]
                    sk = "pset%d" % (i % 2)
                    wp, zt1, zt2, xx = W["wp"], W["zt1"], W["zt2"], W["xx"]
                    OP("dve", "tensor_tensor", wp.rearrange("p (k c n) -> p k c n", k=2, n=128), W["w16l"].rearrange("p (k c n) -> p k c n", k=2, n=128),
                       CINv[:, :, t, tb * TPB:(tb + 1) * TPB].unsqueeze(3).to_broadcast([128, 2, TPB, 128]), ALU.add,
                       reads=["w16l%d" % (i % 2), "CIN", sk], writes=[sk])
                    OP("dve", "tensor_tensor", v4(zt1), v4(wp[:, 0:TB]), bc4(Ep_re[:, t, :]), ALU.mult, reads=[sk] + dk, writes=[sk])
                    OP("pool", "tensor_tensor", v4(zt2), v4(wp[:, TB:2 * TB]), bc4(nEp_im[:, t, :]), ALU.mult, reads=[sk] + dk, writes=[sk])
                    OP("pool", "tensor_tensor", xx[:, 0:TB], zt1, zt2, ALU.add, reads=[sk], writes=[sk])
                    OP("dve", "tensor_tensor", v4(zt1), v4(wp[:, 0:TB]), bc4(nEp_im[:, t, :]), ALU.mult, reads=[sk] + dk, writes=[sk])
                    OP("dve", "tensor_tensor", v4(zt2), v4(wp[:, TB:2 * TB]), bc4(Ep_re[:, t, :]), ALU.mult, reads=[sk] + dk, writes=[sk])
                    OP("pool", "tensor_tensor", xx[:, TB:2 * TB], zt1, zt2, ALU.subtract, reads=[sk], writes=[sk])
                    py = ps[4 + ct % 2]
                    pyk = "ps%d" % (4 + ct % 2)
                    cb_ = Cblk[ct % 2]
                    cbk = "Cblk%d" % (ct % 2)
                    tl = t % 4
                    OP("pe", "matmul", py[:, 0:TB], cb_[:, tl * 128:(tl + 1) * 128], xx[:, 0:TB], start=(tl == 0), stop=False,
                       reads=[cbk, sk], writes=[pyk])
                    OP("pe", "matmul", py[:, 0:TB], cb_[:, 512 + tl * 128:512 + (tl + 1) * 128], xx[:, TB:2 * TB], start=False, stop=(tl == 3),
                       reads=[cbk, sk], writes=[pyk])
                    if tl == 3:
                        us = usl[tb % 2]
                        uk = "usl%d" % (tb % 2)
                        usv = us.rearrange("p (c n) -> p c n", c=4)
                        OP("dve", "scalar_tensor_tensor", ysg[:, ct * TB:(ct + 1) * TB], usv[:, ct, :], s5d[:, ct:ct + 1], py[:, 0:TB], ALU.mult, ALU.add,
                           reads=[uk, "s5d", pyk, "ysg"], writes=["ysg"])
                        OP("act", "activation", ysg[:, ct * TB:(ct + 1) * TB], ysg[:, ct * TB:(ct + 1) * TB], AF.Gelu, reads=["ysg"], writes=["ysg"])
                        OP("act", "activation", ysb[:, ct * TB:(ct + 1) * TB], ysg[:, ct * TB:(ct + 1) * TB], AF.Copy, reads=["ysg", "ysb"], writes=["ysb"])
                    if t == 15:
                        for oc in range(4):
                            for kc in range(4):
                                OP("pe", "matmul", ps[6][:, 0:TB], gluwv[:, kc, oc * 128:(oc + 1) * 128], ysb[:, kc * TB:(kc + 1) * TB], start=(kc == 0), stop=(kc == 3),
                                   reads=["gluw", "ysb"], writes=["ps6"])
                            OP("act", "activation", sgb, ps[6][:, 0:TB], AF.Sigmoid, bias=glub[:, oc:oc + 1], reads=["ps6", "glub", "sgb"], writes=["sgb"])
                            OP("dve", "tensor_tensor", AT[:, 12 + oc, tb * TB:(tb + 1) * TB], ysg[:, oc * TB:(oc + 1) * TB], sgb, ALU.mult,
                               reads=["ysg", "sgb"], writes=["AT"])

            def rest():
                def ln_phase(src, iw, dst_tiles, to_AT, is_out):
                    S.barrier()
                    ar.reset()
                    sm_off[0] = 0
                    lnw = ar.f32(D)
                    lnb = ar.f32(D)
                    S.dma("sp", lnw, P["lnp"][iw], "lnw", writes=["lnw"])
                    S.dma("sp", lnb, P["lnp"][iw + 1], "lnb", writes=["lnb"])
                    rs = [ar.f32(D), ar.f32(D)]
                    xo = [ar.f32(D), ar.f32(D)]
                    stt = [sm(24), sm(24)]
                    mvv = [sm(4), sm(4)]
                    for tt in range(NT):
                        sl = tt % 2
                        rk_, xk, lk = "rs%d" % sl, "xo%d" % sl, "lst%d" % sl
                        S.dma("sp", rs[sl], src[tt], rk_, reads=[(src.name if hasattr(src, "name") else "src", tt)], writes=[rk_])
                        for c in range(4):
                            OP("dve", "bn_stats", stt[sl][:, c * 6:(c + 1) * 6], rs[sl][:, c * 512:(c + 1) * 512], reads=[rk_, lk], writes=[lk])
                        OP("dve", "bn_aggr", mvv[sl][:, 0:2], stt[sl], reads=[lk], writes=[lk])
                        OP("act", "activation", mvv[sl][:, 2:3], mvv[sl][:, 1:2], AF.Sqrt, bias=EPS, reads=[lk], writes=[lk])
                        OP("dve", "reciprocal", mvv[sl][:, 2:3], mvv[sl][:, 2:3], reads=[lk], writes=[lk])
                        OP("dve", "scalar_tensor_tensor", mvv[sl][:, 3:4], mvv[sl][:, 0:1], -1.0, mvv[sl][:, 2:3], ALU.mult, ALU.mult, reads=[lk], writes=[lk])
                        OP("act", "activation", xo[sl], rs[sl], AF.Identity, scale=mvv[sl][:, 2:3], bias=mvv[sl][:, 3:4], reads=[rk_, lk, xk], writes=[xk])
                        OP("dve", "tensor_tensor", xo[sl], xo[sl], lnw, ALU.mult, reads=[xk, "lnw"], writes=[xk])
                        OP("pool", "tensor_tensor", xo[sl], xo[sl], lnb, ALU.add, reads=[xk, "lnb"], writes=[xk])
                        S.dma("sp", dst_tiles(tt), xo[sl], xk, reads=[xk], writes=[("lnout", iw, tt)], is_output=is_out)
                        if to_AT:
                            for g in range(4):
                                b = ps[g]
                                for j in range(4):
                                    kc = 4 * g + j
                                    OP("pe", "transpose", b[:, j * 128:(j + 1) * 128], xo[sl][:, kc * 128:(kc + 1) * 128], identf,
                                       reads=[xk, "CST"], writes=["ps%d" % g])
                                dstv = AT[:, 4 * g:4 * g + 4, tt * 128:(tt + 1) * 128]
                                srcv = b[:].rearrange("p (a b) -> p a b", a=4)
                                if g % 2 == 0:
                                    OP("act", "activation", dstv, srcv, AF.Copy, reads=["ps%d" % g], writes=["AT"])
                                else:
                                    OP("dve", "tensor_copy", dstv, srcv, reads=["ps%d" % g], writes=["AT"])

                S.barrier()
                ar.reset()
                wsl = [ar.bf16(16 * 512), ar.bf16(16 * 512)]
                xres = [ar.f32(512), ar.f32(512)]
                rstg = [ar.f32(512), ar.f32(512)]
                it = 0
                for g in range(4):
                    wv, wkey = load_w(wsl, g, P["w_out"], g * 512, 512, 16)
                    for tt in range(NT):
                        b, sl = it % 4, it % 2
                        it += 1
                        pa, pk = ps[b], "ps%d" % b
                        S.dma("sp", xres[sl], x_tiles(tt)[:, g * 512:(g + 1) * 512], "xres%d" % sl, writes=["xres%d" % sl])
                        for kc in range(16):
                            OP("pe", "matmul", pa[:, 0:512], AT[:, kc, tt * 128:(tt + 1) * 128], wv[:, kc, :], start=(kc == 0), stop=(kc == 15),
                               reads=["AT", wkey], writes=[pk])
                        OP("dve", "scalar_tensor_tensor", rstg[sl], xres[sl], ALPHA, pa[:, 0:512], ALU.mult, ALU.add, reads=["xres%d" % sl, pk, "rstg%d" % sl],
                           writes=["rstg%d" % sl])
                        S.dma("sp", R1[tt, :, g * 512:(g + 1) * 512], rstg[sl], "rstg%d" % sl, reads=["rstg%d" % sl], writes=[("R1", tt)])
                ln_phase(R1, 0, lambda tt: X1[tt], True, False)

                S.barrier()
                ar.reset()
                wsl = [ar.bf16(16 * 512), ar.bf16(16 * 512)]
                hrelu = [ar.f32(TB), ar.f32(TB)]
                hstg = [ar.bf16(4 * T), ar.bf16(4 * T)]
                it = 0
                for g in range(16):
                    wv, wkey = load_w(wsl, g, P["w_up"], g * 512, 512, 16)
                    hs = hstg[g % 2]
                    hk = "hstg%d" % (g % 2)
                    for fi in range(4):
                        for tb in range(NTB):
                            b, sl = it % 4, it % 2
                            it += 1
                            pa, pk = ps[b], "ps%d" % b
                            for kc in range(16):
                                OP("pe", "matmul", pa[:, 0:TB], wv[:, kc, fi * 128:(fi + 1) * 128], AT[:, kc, tb * TB:(tb + 1) * TB],
                                   start=(kc == 0), stop=(kc == 15), reads=["AT", wkey], writes=[pk])
                            OP("act", "activation", hrelu[sl], pa[:, 0:TB], AF.Relu, reads=[pk, "hrelu%d" % sl], writes=["hrelu%d" % sl])
                            OP("dve", "tensor_tensor", hs[:, fi * T + tb * TB:fi * T + (tb + 1) * TB], hrelu[sl], hrelu[sl], ALU.mult,
                               reads=["hrelu%d" % sl, hk], writes=[hk])
                    for k4 in range(4):
                        S.dma("sp", HH[:, :, 4 * g + k4, :].rearrange("t p d -> p t d"), hs[:, k4 * T:(k4 + 1) * T].rearrange("p (t d) -> p t d", d=128), hk,
                              reads=[hk], writes=["HH"])
                S.barrier()
                ar.reset()
                if 16 * T >= 64 * 512:
                    wdl = [ar.bf16(64 * 512), AT[:].rearrange("p a t -> p (a t)")[:, 0:64 * 512]]
                else:
                    wdl = [ar.bf16(64 * 512)]
                hsl = [ar.bf16(64 * 128), ar.bf16(64 * 128)]
                x1s = [ar.f32(512), ar.f32(512)]
                r2s = [ar.f32(512), ar.f32(512)]
                it = 0
                for g in range(4):
                    wv, wkey = load_w(wdl, g, P["w_down"], g * 512, 512, 64, key="wd")
                    for tt in range(NT):
                        b, sl = it % 4, it % 2
                        it += 1
                        pa, pk = ps[b], "ps%d" % b
                        S.dma("sp", hsl[sl], HH[tt].rearrange("p k d -> p (k d)"), "hsl%d" % sl, reads=["HH"], writes=["hsl%d" % sl])
                        S.dma("act", x1s[sl], X1[tt, :, g * 512:(g + 1) * 512], "x1s%d" % sl, reads=[("lnout", 0, tt)], writes=["x1s%d" % sl])
                        hv = hsl[sl].rearrange("p (k d) -> p k d", k=64)
                        for kc in range(64):
                            OP("pe", "matmul", pa[:, 0:512], hv[:, kc, :], wv[:, kc, :], start=(kc == 0), stop=(kc == 63),
                               reads=["hsl%d" % sl, wkey], writes=[pk])
                        OP("dve", "scalar_tensor_tensor", r2s[sl], x1s[sl], ALPHA, pa[:, 0:512], ALU.mult, ALU.add, reads=["x1s%d" % sl, pk, "r2s%d" % sl],
                           writes=["r2s%d" % sl])
                        S.dma("act", R2[tt, :, g * 512:(g + 1) * 512], r2s[sl], "r2s%d" % sl, reads=["r2s%d" % sl], writes=[("R2", tt)])

                load_AT_from(lambda tt: R2[tt])
                S.barrier()
                ar.reset()
                pst = [ar.f32(PLE), ar.f32(PLE)]
                for tt in range(NT):
                    sl = tt % 2
                    S.dma("sp", pst[sl], P["p"][tt * 128:(tt + 1) * 128, :], "pst%d" % sl, writes=["pst%d" % sl])
                    for kc in range(2):
                        OP("pe", "transpose", ps[4][:, kc * 128:(kc + 1) * 128], pst[sl][:, kc * 128:(kc + 1) * 128], identf, reads=["pst%d" % sl, "CST"],
                           writes=["ps4"])
                    OP("dve", "tensor_copy", ATP[:, :, tt * 128:(tt + 1) * 128], ps[4][:, 0:256].rearrange("p (a b) -> p a b", a=2), reads=["ps4"], writes=["ATP"])
                wsl = [ar.bf16(16 * 512), ar.bf16(16 * 512)]
                wpl = ar.bf16(2 * D)
                wplv = wpl.rearrange("p (k n) -> p k n", k=2)
                S.dma("pool", wplv, P["w_ple"].rearrange("(k p) n -> p k n", p=128), "wpl", writes=["wpl"])
                sgs = [ar.f32(512), ar.f32(512)]
                r2l = [ar.f32(512), ar.f32(512)]
                r3s = [ar.f32(512), ar.f32(512)]
                it = 0
                for g in range(4):
                    wv, wkey = load_w(wsl, g, P["w_gate"], g * 512, 512, 16)
                    for tt in range(NT):
                        sl = it % 2
                        pa, pk = ps[2 * (it % 2)], "ps%d" % (2 * (it % 2))
                        pb, pbk = ps[2 * (it % 2) + 1], "ps%d" % (2 * (it % 2) + 1)
                        it += 1
                        S.dma("sp", r2l[sl], R2[tt, :, g * 512:(g + 1) * 512], "r2l%d" % sl, reads=[("R2", tt)], writes=["r2l%d" % sl])
                        for kc in range(16):
                            OP("pe", "matmul", pa[:, 0:512], AT[:, kc, tt * 128:(tt + 1) * 128], wv[:, kc, :], start=(kc == 0), stop=(kc == 15),
                               reads=["AT", wkey], writes=[pk])
                        for kc in range(2):
                            OP("pe", "matmul", pb[:, 0:512], ATP[:, kc, tt * 128:(tt + 1) * 128], wplv[:, kc, g * 512:(g + 1) * 512], start=(kc == 0), stop=(kc == 1),
                               reads=["ATP", "wpl"], writes=[pbk])
                        OP("act", "activation", sgs[sl], pa[:, 0:512], AF.Sigmoid, reads=[pk, "sgs%d" % sl], writes=["sgs%d" % sl])
                        OP("dve", "tensor_tensor", sgs[sl], pb[:, 0:512], sgs[sl], ALU.mult, reads=[pbk, "sgs%d" % sl], writes=["sgs%d" % sl])
                        OP("pool", "tensor_tensor", r3s[sl], sgs[sl], r2l[sl], ALU.add, reads=["sgs%d" % sl, "r2l%d" % sl, "r3s%d" % sl], writes=["r3s%d" % sl])
                        S.dma("sp", R3[tt, :, g * 512:(g + 1) * 512], r3s[sl], "r3s%d" % sl, reads=["r3s%d" % sl], writes=[("R3", tt)])
                ln_phase(R3, 2, out_tiles, False, out_is_ext)

            if stage == "A":
                pm(False)
                s5_a()
            elif stage == "B":
                pm(True)
                s5_a()
                s5_b()
                rest()
            else:
                pm(False)
                s5_a()
                OP("pool", "collective_compute", "AllGather", ALU.bypass, replica_groups=[list(range(NCORES))],
                   ins=[ccsrc[l].opt()], outs=[ccdst[l].opt()], reads=[("ccsrc", l)], writes=[("ccdst", l)])
                s5_b()
                pm(True)
                rest()

        if NL == 1:
            layer(0, lambda tt: x_in[tt * 128:(tt + 1) * 128, :], xh_in, (lambda tt: outs["y"][tt * 128:(tt + 1) * 128, :]) if stage != "A" else None, True)
        else:
            layer(0, lambda tt: x_in[tt * 128:(tt + 1) * 128, :], xh_in, lambda tt: XB[tt], False)
            S.barrier()
            ar.reset()
            hb = ar.f32(D)
            hg = ar.f32(NCORES * D)
            hacc = ar.f32(D)
            OHM = ar.f32(NCORES)
            S.dma("sp", OHM, oh_in, "OHM", writes=["OHM"])
            S.dma("sp", hb[0:4, :], XB[NT - 1, 124:128, :], "hb", writes=["hb"])
            S.dma("sp", hsrc, hb[0:4, :], "hb", reads=["hb"], writes=["hsrc"])
            OP("pool", "collective_compute", "AllGather", ALU.bypass, replica_groups=[list(range(NCORES))],
               ins=[hsrc.opt()], outs=[hdst.opt()], reads=["hsrc"], writes=["hdst"])
            S.dma("sp", hg[0:4, :].rearrange("p (r n) -> p r n", r=NCORES), hdst.rearrange("(r p) n -> p r n", p=4), "hg", reads=["hdst"], writes=["hg"])
            OP("dve", "memset", hacc[0:4, :], 0.0, writes=["hacc"])
            for j in range(NCORES):
                OP("dve", "scalar_tensor_tensor", hacc[0:4, :], hg[0:4, j * D:(j + 1) * D], OHM[0:4, j:j + 1], hacc[0:4, :], ALU.mult, ALU.add,
                   reads=["hg", "OHM", "hacc"], writes=["hacc"])
            S.dma("sp", halo1, hacc[0:4, :], "hacc", reads=["hacc"], writes=["halo1"])
            layer(1, lambda tt: XB[tt], halo1, lambda tt: outs["y"][tt * 128:(tt + 1) * 128, :], True)
        S.finish()
        S.emit()
    return nc, outs, dbg_out


def host_consts():
    c = np.zeros((128, 128 * 4 + 64 + 18), np.float32)
    c[:, 0:128] = np.eye(128, dtype=np.float32)
    j = np.arange(128)[:, None]
    i = np.arange(128)[None, :]
    c[:, 128:256] = (j <= i).astype(np.float32)
    c[:, 256:384] = 1.0
    invf = (np.float32(10000.0) ** (-(np.arange(0, 128, 2, dtype=np.float32) / np.float32(128.0)))).astype(np.float32)
    c[:, 512:576] = invf[None, :]
    lg = np.log(1.0 - 2.0 ** (-5.0 - np.arange(6, dtype=np.float64)))
    idx = np.arange(128, dtype=np.float64)[:, None]
    c[:, 576:582] = np.exp(lg[None, :] * (idx + 1.0))
    c[:, 582:588] = np.exp(-lg[None, :] * (idx + 1.0)) * 128.0 ** -0.5
    c[:, 588:594] = np.exp(lg[None, :] * 128.0)
    return c


def _s5_state_layout(a):
    return np.ascontiguousarray(a.reshape(16, 2, 64).transpose(1, 2, 0).reshape(128, 16))


def host_layer_arrays(inp, l):
    f = np.float32
    rep = lambda v: np.ascontiguousarray(np.broadcast_to(np.asarray(v, f)[None, :], (128, np.asarray(v).shape[-1])))
    d = {}
    d["w_in"] = np.asarray(inp["w_in"][l], f)
    cw = np.asarray(inp["mlstm_conv_w"][l], f)
    d["cw"] = np.ascontiguousarray(cw.reshape(4, 12, 128).transpose(2, 1, 0))
    d["cb"] = np.ascontiguousarray(np.asarray(inp["mlstm_conv_b"][l], f).reshape(12, 128).T)
    d["gbias"] = rep(np.concatenate([np.asarray(inp["mlstm_i_bias"][l], f), np.asarray(inp["mlstm_f_bias"][l], f)]))
    d["hnw"] = rep(np.concatenate([np.asarray(inp["ret_norm_w"][l], f), np.asarray(inp["mlstm_norm_w"][l], f)]))
    lre = np.asarray(inp["s5_lambda_re"][l], f)
    lim = np.asarray(inp["s5_lambda_im"][l], f)
    ldt = np.broadcast_to(np.asarray(inp["s5_log_dt"][l], f)[:, None], (32, 64))
    s5p = np.stack([_s5_state_layout(lre), _s5_state_layout(lim), _s5_state_layout(ldt)], axis=1)
    d["s5p"] = np.ascontiguousarray(s5p)
    flat = lambda a: np.ascontiguousarray(_s5_state_layout(a).T.reshape(2048))
    s5rep = np.stack([flat(lre), flat(lim), flat(ldt)], axis=0)
    d["s5rep"] = np.ascontiguousarray(np.broadcast_to(s5rep[None], (128, 3, 2048)))
    Bb = np.zeros((2, 128, 16, 128), f)
    Cb = np.zeros((2, 128, 16, 128), f)
    Bs = [np.asarray(inp["s5_B_re"][l], f), np.asarray(inp["s5_B_im"][l], f)]
    Cs = [np.asarray(inp["s5_C_re"][l], f), np.asarray(inp["s5_C_im"][l], f)]
    for t in range(16):
        for s in range(2):
            g = 2 * t + s
            r0 = (g % 8) * 16
            for k in range(2):
                Bb[k, r0:r0 + 16, t, s * 64:(s + 1) * 64] = Bs[k][g].T
                Cb[k, s * 64:(s + 1) * 64, t, r0:r0 + 16] = Cs[k][g].T
    d["s5B"] = Bb
    d["s5C"] = Cb
    d["s5D"] = np.ascontiguousarray(np.asarray(inp["s5_D"][l], f).reshape(4, 128).T)
    d["glu_w"] = np.asarray(inp["s5_glu_w"][l], f)
    d["glu_b"] = np.ascontiguousarray(np.asarray(inp["s5_glu_b"][l], f).reshape(4, 128).T)
    d["w_out"] = np.asarray(inp["w_out"][l], f)
    d["lnp"] = np.ascontiguousarray(np.stack([rep(inp["ln1_w"][l]), rep(inp["ln1_b"][l]), rep(inp["ln2_w"][l]), rep(inp["ln2_b"][l])], 0))
    d["w_up"] = np.asarray(inp["w_up"][l], f)
    d["w_down"] = np.asarray(inp["w_down"][l], f)
    d["w_gate"] = np.asarray(inp["w_gate"][l], f)
    d["w_ple"] = np.asarray(inp["w_ple"][l], f)
    return d


_PROGS = {}


def _prog(stage):
    if stage not in _PROGS:
        _PROGS[stage] = build(SEQ // NCORES // 128, 1, stage)[0]
    return _PROGS[stage]


A_KEYS = ("w_in", "cw", "cb", "gbias", "s5p", "s5rep", "s5B")


def kernel(**inputs):
    inp = {k: np.asarray(v) for k, v in inputs.items()}
    x = np.ascontiguousarray(inp["x"][0], dtype=np.float32)
    pos = np.asarray(inp["positions"][0], dtype=np.int32)
    TC = SEQ // NCORES
    NT = TC // 128
    consts = host_consts()
    if "F" not in _PROGS:
        _PROGS["F"] = build(NT, 2, "F")[0]
    nc = _PROGS["F"]
    las = [host_layer_arrays(inp, l) for l in range(2)]
    maps = []
    for c in range(NCORES):
        cm = np.zeros((128, NCORES), np.float32)
        cm[:, :c] = 1.0
        oh = np.zeros((128, NCORES), np.float32)
        if c > 0:
            oh[:, c - 1] = 1.0
        m = {"x": np.ascontiguousarray(x[c * TC:(c + 1) * TC]),
             "xh": np.ascontiguousarray(x[c * TC - 4:c * TC]) if c > 0 else np.zeros((4, D), np.float32),
             "pos": np.ascontiguousarray(pos[c * TC:(c + 1) * TC].reshape(NT, 128).T), "cmask": cm, "oh": oh, "consts": consts}
        for l in range(2):
            m["p%d" % l] = np.ascontiguousarray(np.asarray(inp["p"][l, 0, c * TC:(c + 1) * TC], np.float32))
            for k, v in las[l].items():
                m[k + str(l)] = v
        maps.append(m)
    res = run_bass_kernel_spmd(nc, maps, core_ids=list(range(NCORES)))
    return np.concatenate([np.asarray(r["y"], np.float32) for r in res.results], 0)[None].astype(np.float32)
```
